# Optimizing a Trainium2 kernel written in Bass

```python
import math
import jax, jax.numpy as jnp
from jax import lax
import numpy as np

D_MODEL = 4096
BATCH = 4
SEQ = 2048
DEPTH = 1

CHUNK = 64
Q_BLOCK = 128

MLA_HEADS = 16
MLA_NOPE_DIM = 128
MLA_ROPE_DIM = 64
MLA_V_DIM = 128
MLA_Q_LORA = 1024
MLA_KV_LORA = 512
ROPE_THETA = 10000.0
MLA_OUT_WIDTH = MLA_HEADS * MLA_V_DIM

HG_HEADS = 16
HG_K_DIM = 128
HG_V_DIM = 128
HG_BLOCK = 16
HG_WIDTH_K = HG_HEADS * HG_K_DIM
HG_WIDTH_V = HG_HEADS * HG_V_DIM

D_FF = -(-8 * D_MODEL // (3 * 256)) * 256

ALPHA = (2 * DEPTH) ** 0.25
BETA = (8 * DEPTH) ** -0.25
EPS = 1e-5

SPLITS = (MLA_Q_LORA, MLA_KV_LORA, MLA_ROPE_DIM,
          HG_WIDTH_K, HG_WIDTH_K, HG_WIDTH_V, HG_WIDTH_V,
          D_MODEL, D_MODEL)
IN_WIDTH = sum(SPLITS)

kernel_name = "mla_hgrn2_gated_hybrid_deepnorm"


def layer_norm(x, g, b):
    xf = x.astype(jnp.float32)
    mu = jnp.mean(xf, axis=-1, keepdims=True)
    var = jnp.mean(jnp.square(xf - mu), axis=-1, keepdims=True)
    return ((xf - mu) * lax.rsqrt(var + EPS) * g + b).astype(x.dtype)


def rms_norm(x, g):
    xf = x.astype(jnp.float32)
    ms = jnp.mean(jnp.square(xf), axis=-1, keepdims=True)
    return (xf * lax.rsqrt(ms + EPS) * g).astype(x.dtype)


def rope(x, positions):
    half = x.shape[-1] // 2
    inv_freq = ROPE_THETA ** (-jnp.arange(half, dtype=jnp.float32) / half)
    ang = positions[..., None].astype(jnp.float32) * inv_freq
    cos = jnp.cos(ang)[:, :, None, :]
    sin = jnp.sin(ang)[:, :, None, :]
    xf = x.astype(jnp.float32)
    x1, x2 = xf[..., :half], xf[..., half:]
    return jnp.concatenate([x1 * cos - x2 * sin, x1 * sin + x2 * cos], axis=-1).astype(x.dtype)


def split_columns(p):
    points = np.cumsum(np.array(SPLITS))[:-1].tolist()
    return jnp.split(p, points, axis=-1)


def mla_attention(q_nope, q_rope, k_nope, k_rope, v):
    T = q_nope.shape[1]
    scale = (MLA_NOPE_DIM + MLA_ROPE_DIM) ** -0.5
    outs = []
    for j in range(T // Q_BLOCK):
        s0, s1 = j * Q_BLOCK, (j + 1) * Q_BLOCK
        scores = (jnp.einsum('bqhd,bkhd->bhqk', q_nope[:, s0:s1], k_nope[:, :s1])
                  + jnp.einsum('bqhr,bkr->bhqk', q_rope[:, s0:s1], k_rope[:, :s1]))
        scores = scores.astype(jnp.float32) * scale
        q_chunk = (s0 + jnp.arange(Q_BLOCK)) // CHUNK
        k_chunk = jnp.arange(s1) // CHUNK
        scores = jnp.where(k_chunk[None, :] <= q_chunk[:, None], scores, -jnp.inf)
        probs = jax.nn.softmax(scores, axis=-1).astype(v.dtype)
        outs.append(jnp.einsum('bhqk,bkhd->bqhd', probs, v[:, :s1]))
    return jnp.concatenate(outs, axis=1)


def hgrn2_chunkwise(q, k, v, log_f):
    B, T, H, K = q.shape
    V = v.shape[-1]
    L = HG_BLOCK
    N = T // L

    def blocks(a):
        a = a.astype(jnp.float32)
        return a.reshape(B, N, L, H, a.shape[-1]).transpose(1, 0, 3, 2, 4)

    qb, kb, vb, gb = blocks(q), blocks(k), blocks(v), blocks(log_f)
    causal = jnp.tril(jnp.ones((L, L), dtype=bool))[:, :, None]

    def step(S, blk):
        qc, kc, vc, gc = blk
        b = jnp.cumsum(gc, axis=-2)
        diff = b[:, :, :, None, :] - b[:, :, None, :, :]
        decay = jnp.exp(jnp.where(causal, diff, -jnp.inf))
        scores = jnp.einsum('bhtk,bhsk,bhtsk->bhts', qc, kc, decay)
        o = (jnp.einsum('bhts,bhsv->bhtv', scores, vc)
             + jnp.einsum('bhtk,bhkv->bhtv', qc * jnp.exp(b), S))
        b_last = b[:, :, -1:, :]
        S_new = (jnp.exp(b_last[:, :, 0, :])[..., None] * S
                 + jnp.einsum('bhsk,bhsv->bhkv', kc * jnp.exp(b_last - b), vc))
        return S_new, o

    S0 = jnp.zeros((B, H, K, V), jnp.float32)
    _, o = lax.scan(step, S0, (qb, kb, vb, gb))
    return o.transpose(1, 0, 3, 2, 4).reshape(B, T, H, V)


def token_mixers(h, positions, w_in, q_norm_g, w_uq, kv_norm_g, w_ukv, lb,
                 hg_norm_g, w_branch_a, w_branch_b, w_out):
    B, T, _ = h.shape
    proj = h @ w_in
    c_q, c_kv, k_rope, hq, hf, hi, hgate, gate_a, gate_b = split_columns(proj)

    q = (rms_norm(c_q, q_norm_g) @ w_uq).reshape(B, T, MLA_HEADS, MLA_NOPE_DIM + MLA_ROPE_DIM)
    q_nope = q[..., :MLA_NOPE_DIM]
    q_rope = rope(q[..., MLA_NOPE_DIM:], positions)
    kv = (rms_norm(c_kv, kv_norm_g) @ w_ukv).reshape(B, T, MLA_HEADS, MLA_NOPE_DIM + MLA_V_DIM)
    k_nope, v = kv[..., :MLA_NOPE_DIM], kv[..., MLA_NOPE_DIM:]
    k_rope = rope(k_rope[:, :, None, :], positions)[:, :, 0, :]
    o_a = mla_attention(q_nope, q_rope, k_nope, k_rope, v).reshape(B, T, MLA_OUT_WIDTH)

    f = lb + (1.0 - lb) * jax.nn.sigmoid(hf.astype(jnp.float32))
    log_f = jnp.log(f).reshape(B, T, HG_HEADS, HG_K_DIM)
    k_in = (1.0 - f).reshape(B, T, HG_HEADS, HG_K_DIM)
    q_hg = jax.nn.silu(hq).reshape(B, T, HG_HEADS, HG_K_DIM)
    i_hg = hi.reshape(B, T, HG_HEADS, HG_V_DIM)
    o_hg = hgrn2_chunkwise(q_hg, k_in, i_hg, log_f)
    g_hg = jax.nn.silu(hgate.astype(jnp.float32)).reshape(B, T, HG_HEADS, HG_V_DIM)
    o_b = (rms_norm(o_hg, hg_norm_g) * g_hg).reshape(B, T, HG_WIDTH_V).astype(h.dtype)

    merged = (jax.nn.sigmoid(gate_a) * (o_a @ w_branch_a)
              + jax.nn.sigmoid(gate_b) * (o_b @ w_branch_b))
    return merged @ w_out


def swiglu(h, w_gate, w_up, w_down):
    return (jax.nn.silu(h @ w_gate) * (h @ w_up)) @ w_down


def setup_inputs(seed: int = 0) -> dict:
    key = jax.random.key(seed)
    ks = jax.random.split(key, 24)

    def normal(k, shape, scale):
        return jax.random.normal(k, shape, jnp.float32) * scale

    def gain(k, shape):
        return 1.0 + normal(k, shape, 0.02)

    x = normal(ks[0], (BATCH, SEQ, D_MODEL), 1.0)
    offsets = jax.random.randint(ks[1], (BATCH, 1), 0, 4096, dtype=jnp.int32)
    positions = offsets + jnp.arange(SEQ, dtype=jnp.int32)[None, :]
    return {
        "x": x,
        "positions": positions,
        "ln_in_g": gain(ks[2], (D_MODEL,)),
        "ln_in_b": normal(ks[3], (D_MODEL,), 0.02),
        "w_in": normal(ks[4], (DEPTH, D_MODEL, IN_WIDTH), D_MODEL ** -0.5),
        "q_norm_g": gain(ks[5], (DEPTH, MLA_Q_LORA)),
        "w_uq": normal(ks[6], (DEPTH, MLA_Q_LORA, MLA_HEADS * (MLA_NOPE_DIM + MLA_ROPE_DIM)), MLA_Q_LORA ** -0.5),
        "kv_norm_g": gain(ks[7], (DEPTH, MLA_KV_LORA)),
        "w_ukv": normal(ks[8], (DEPTH, MLA_KV_LORA, MLA_HEADS * (MLA_NOPE_DIM + MLA_V_DIM)), MLA_KV_LORA ** -0.5),
        "hg_lb": normal(ks[9], (DEPTH + 1, HG_WIDTH_K), 0.1),
        "hg_norm_g": gain(ks[10], (DEPTH, HG_V_DIM)),
        "w_branch_a": normal(ks[11], (DEPTH, MLA_OUT_WIDTH, D_MODEL), MLA_OUT_WIDTH ** -0.5),
        "w_branch_b": normal(ks[12], (DEPTH, HG_WIDTH_V, D_MODEL), HG_WIDTH_V ** -0.5),
        "w_out": normal(ks[13], (DEPTH, D_MODEL, D_MODEL), BETA * D_MODEL ** -0.5),
        "ln1_g": gain(ks[14], (DEPTH, D_MODEL)),
        "ln1_b": normal(ks[15], (DEPTH, D_MODEL), 0.02),
        "w_gate": normal(ks[16], (DEPTH, D_MODEL, D_FF), D_MODEL ** -0.5),
        "w_up": normal(ks[17], (DEPTH, D_MODEL, D_FF), D_MODEL ** -0.5),
        "w_down": normal(ks[18], (DEPTH, D_FF, D_MODEL), BETA * D_FF ** -0.5),
        "ln2_g": gain(ks[19], (DEPTH, D_MODEL)),
        "ln2_b": normal(ks[20], (DEPTH, D_MODEL), 0.02),
    }


def reference(x, positions, ln_in_g, ln_in_b, w_in, q_norm_g, w_uq, kv_norm_g, w_ukv,
              hg_lb, hg_norm_g, w_branch_a, w_branch_b, w_out, ln1_g, ln1_b,
              w_gate, w_up, w_down, ln2_g, ln2_b):
    h = layer_norm(x, ln_in_g, ln_in_b)
    lb_all = jnp.cumsum(jax.nn.softmax(hg_lb.astype(jnp.float32), axis=0), axis=0)
    for l in range(DEPTH):
        mix = token_mixers(h, positions, w_in[l], q_norm_g[l], w_uq[l], kv_norm_g[l], w_ukv[l],
                           lb_all[l], hg_norm_g[l], w_branch_a[l], w_branch_b[l], w_out[l])
        h = layer_norm(ALPHA * h + mix, ln1_g[l], ln1_b[l])
        h = layer_norm(ALPHA * h + swiglu(h, w_gate[l], w_up[l], w_down[l]), ln2_g[l], ln2_b[l])
    return h
```

```python
import math
import numpy as np
from contextlib import ExitStack
import concourse.bass as bass
import concourse.mybir as mybir
from concourse.bass_utils import run_bass_kernel_spmd

F32 = mybir.dt.float32
BF16 = mybir.dt.bfloat16
I32 = mybir.dt.int32
ALU = mybir.AluOpType
AF = mybir.ActivationFunctionType

EPS = 1e-5
ALPHA = 2.0 ** 0.25
NOPE, ROPE, VD = 128, 64, 128
HCH = 64
ARENA_BYTES = 206 * 1024
NSLOT = 4
SLOT_BYTES = 8192


class Buf:
    __slots__ = ("name", "writer", "rd", "rd_dma", "sem", "semval")

    def __init__(self, name):
        self.name = name
        self.writer = None
        self.rd = {}
        self.rd_dma = []
        self.sem = None
        self.semval = 0


class Op:
    __slots__ = ("eng", "emit", "deps", "ticket", "needs_inc", "is_dma", "dma_sem", "dma_val")

    def __init__(self, eng, emit):
        self.eng = eng
        self.emit = emit
        self.deps = []
        self.ticket = 0
        self.needs_inc = False
        self.is_dma = False
        self.dma_sem = None
        self.dma_val = 0


class Prog:
    ENGS = ("pe", "act", "dve", "pool", "sp")

    def __init__(self, nc, stack):
        self.nc = nc
        self.stack = stack
        self.streams = {e: [] for e in self.ENGS}
        self.esem = {e: stack.enter_context(nc.semaphore("es_" + e)) for e in self.ENGS}
        self.last = {e: None for e in self.ENGS}
        self.dmas = []
        self.nsem = 0
        self.nops = 0

    def _deps(self, op, reads, writes):
        deps = {}
        for b in reads:
            if b.writer is not None:
                deps[id(b.writer)] = b.writer
        for b in writes:
            if b.writer is not None:
                deps[id(b.writer)] = b.writer
            for r in b.rd.values():
                deps[id(r)] = r
            for r in b.rd_dma:
                deps[id(r)] = r
        out = []
        for p in deps.values():
            if p is op:
                continue
            if op.eng == "pe" and p.eng == "pe" and not p.is_dma and not op.is_dma:
                continue
            out.append(p)
            if not p.is_dma:
                p.needs_inc = True
        op.deps = out
        for b in writes:
            b.writer = op
            b.rd = {}
            b.rd_dma = []
        for b in reads:
            if op.is_dma:
                b.rd_dma.append(op)
            else:
                b.rd[op.eng] = op

    def op(self, eng, emit, reads=(), writes=()):
        o = Op(eng, emit)
        self._deps(o, reads, writes)
        self.streams[eng].append(o)
        self.last[eng] = o
        self.nops += 1
        return o

    def dma(self, queue, emits, home, reads=(), writes=()):
        o = Op(queue, emits)
        o.is_dma = True
        if home.sem is None:
            home.sem = self.stack.enter_context(self.nc.semaphore("ds%d" % self.nsem))
            self.nsem += 1
        self._deps(o, reads, writes)
        home.semval += 16 * len(emits)
        o.dma_sem = home.sem
        o.dma_val = home.semval
        self.streams[queue].append(o)
        self.dmas.append(o)
        self.nops += 1
        return o

    def barrier(self, engs=("pe", "act", "dve", "sp")):
        lasts = [self.last[e] for e in engs if self.last[e] is not None]
        dmas = [d for d in self.dmas if d.eng != "pool"]
        self.dmas = []
        new = {}
        for e in engs:
            o = Op(e, None)
            o.deps = [p for p in lasts if not p.is_dma] + dmas
            for p in o.deps:
                if not p.is_dma:
                    p.needs_inc = True
            self.streams[e].append(o)
            new[e] = o

    def wait_all(self, eng, ops):
        o = Op(eng, None)
        o.deps = list(ops)
        for p in ops:
            if not p.is_dma:
                p.needs_inc = True
        self.streams[eng].append(o)

    def emit_all(self):
        nc = self.nc
        for e in self.ENGS:
            c = 0
            for o in self.streams[e]:
                if o.needs_inc and not o.is_dma and o.emit is not None:
                    c += 1
                    o.ticket = c
        esem = self.esem
        streams = self.streams

        def run(ename, eng):
            waited = {}
            for o in streams[ename]:
                need = {}
                for p in o.deps:
                    if p.is_dma:
                        s, v = p.dma_sem, p.dma_val
                    else:
                        s, v = esem[p.eng], p.ticket
                    k = id(s)
                    if k not in need or need[k][1] < v:
                        need[k] = (s, v)
                for k, (s, v) in need.items():
                    if waited.get(k, 0) >= v:
                        continue
                    eng.wait_ge(s, v)
                    waited[k] = v
                if o.emit is None:
                    continue
                if o.is_dma:
                    for em in o.emit:
                        em(eng).then_inc(o.dma_sem, 16)
                else:
                    ins = o.emit(eng)
                    if o.needs_inc:
                        ins.then_inc(esem[ename], 1)

        with nc.Block() as block:
            @block.tensor
            def _(eng):
                run("pe", eng)

            @block.scalar
            def _(eng):
                run("act", eng)

            @block.vector
            def _(eng):
                run("dve", eng)

            @block.gpsimd
            def _(eng):
                run("pool", eng)

            @block.sync
            def _(eng):
                run("sp", eng)


class Cfg:
    def __init__(s, T, D, QL, KVL, H, HH, DFF):
        s.T, s.D, s.QL, s.KVL, s.H, s.HH, s.DFF = T, D, QL, KVL, H, HH, DFF
        s.DC, s.QC, s.KVC, s.DFC = D // 128, QL // 128, KVL // 128, DFF // 128
        s.NT, s.NH = T // 128, T // 512
        s.o_cq = 0
        s.o_ckv = QL
        s.o_kr = QL + KVL
        s.o_hq = s.o_kr + ROPE
        s.o_hf = s.o_hq + HH * 128
        s.o_hi = s.o_hf + HH * 128
        s.o_hg = s.o_hi + HH * 128
        s.o_ga = s.o_hg + HH * 128
        s.o_gb = s.o_ga + D
        s.NIN = s.o_gb + D
        assert T % 512 == 0 and D % 512 == 0 and QL % 256 == 0 and KVL % 128 == 0 and DFF % 256 == 0


def kch(KC, KG):
    return [(k0, min(KG, KC - k0)) for k0 in range(0, KC, KG)]


class Builder:
    def __init__(s, cfg, debug=()):
        s.c = cfg
        s.debug = set(debug)
        s.nc = bass.Bass("TRN2", target_bir_lowering=False)
        s.st = ExitStack()
        s.P = Prog(s.nc, s.st)
        s.ev = 0
        s.slot_ctr = 0
        s.bufcache = {}

    def dram_in(s, name, shape, dt=F32):
        return s.nc.dram_tensor(name, list(shape), dt, kind="ExternalInput").ap()

    def dram_tmp(s, name, shape, dt):
        kind = "ExternalOutput" if name in s.debug else "Internal"
        return s.nc.dram_tensor(name, list(shape), dt, kind=kind).ap()

    def view(s, off, shape, dt, parts=128):
        n = 1
        for d in shape[1:]:
            n *= d
        assert off % 4 == 0
        if dt == BF16:
            a = s.arena[:, off // 2: off // 2 + n]
            nb = 2 * n
        else:
            a = s.arena[:, off // 2: off // 2 + 2 * n].bitcast(dt)
            nb = 4 * n
        assert off + nb <= ARENA_BYTES, (off, nb)
        if len(shape) == 3:
            a = a.rearrange("p (a b) -> p a b", a=shape[1])
        elif len(shape) == 4:
            a = a.rearrange("p (a b c) -> p a b c", a=shape[1], b=shape[2])
        if shape[0] != 128:
            a = a[0:shape[0]]
        return a

    def mm(s, out, lhsT, rhs, start, stop, reads, writes, skip=False):
        if skip:
            return s.P.op("pe", lambda e: e.matmul(out, lhsT=lhsT, rhs=rhs, start=start, stop=stop,
                                                   skip_group_check=True), reads, writes)
        return s.P.op("pe", lambda e: e.matmul(out, lhsT=lhsT, rhs=rhs, start=start, stop=stop), reads, writes)

    def tr(s, out, in_, ident, reads, writes):
        return s.P.op("pe", lambda e: e.transpose(out, in_, ident), reads, writes)

    def act(s, out, in_, func, reads, writes, bias=None, scale=None):
        kw = {}
        if bias is not None:
            kw["bias"] = bias
        if scale is not None:
            kw["scale"] = scale
        return s.P.op("act", lambda e: e.activation(out=out, in_=in_, func=func, **kw), reads, writes)

    def tt(s, out, in0, in1, op, reads, writes, eng="dve"):
        return s.P.op(eng, lambda e: e.tensor_tensor(out=out, in0=in0, in1=in1, op=op), reads, writes)

    def ts(s, out, in0, s1, s2, op0, op1, reads, writes):
        if s2 is None:
            return s.P.op("dve", lambda e: e.tensor_scalar(out=out, in0=in0, scalar1=s1, scalar2=None, op0=op0),
                          reads, writes)
        return s.P.op("dve", lambda e: e.tensor_scalar(out=out, in0=in0, scalar1=s1, scalar2=s2, op0=op0, op1=op1),
                      reads, writes)

    def stt(s, out, in0, scalar, in1, op0, op1, reads, writes):
        return s.P.op("dve", lambda e: e.scalar_tensor_tensor(out=out, in0=in0, scalar=scalar, in1=in1,
                                                              op0=op0, op1=op1), reads, writes)

    def copy(s, out, in_, reads, writes):
        s.ev += 1
        if s.ev % 2 == 0:
            return s.P.op("act", lambda e: e.activation(out=out, in_=in_, func=AF.Copy), reads, writes)
        return s.P.op("dve", lambda e: e.tensor_copy(out=out, in_=in_), reads, writes)

    def dma1(s, queue, out, in_, home, reads, writes):
        return s.P.dma(queue, [lambda e: e.dma_start(out=out, in_=in_)], home, reads, writes)

    def dmag(s, queue, pairs, home, reads, writes):
        ems = [(lambda o_, i_: (lambda e: e.dma_start(out=o_, in_=i_)))(o_, i_) for (o_, i_) in pairs]
        return s.P.dma(queue, ems, home, reads, writes)

    def gran_load(s, pieces, KG, GC):
        slot = s.slot_ctr % NSLOT
        s.slot_ctr += 1
        assert KG * GC * 2 <= SLOT_BYTES
        v = s.view(slot * SLOT_BYTES, [128, KG, GC], BF16)
        b = s.slot_bufs[slot]
        ems = []
        for (dc, W, kc0, nkc, sc, ncol) in pieces:
            wv = W.rearrange("(kc p) n -> p kc n", p=128)
            for a in range(0, nkc, 8):
                n_ = min(8, nkc - a)
                ems.append((lambda o_, i_: (lambda e: e.dma_start(out=o_, in_=i_)))(
                    v[:, a:a + n_, dc:dc + ncol], wv[:, kc0 + a:kc0 + a + n_, sc:sc + ncol]))
        s.P.dma("pool", ems, b, writes=[b])
        return v, b

    class GStream:
        def __init__(g, B, specs):
            g.B, g.specs, g.loaded, g.q = B, specs, 0, []
            for _ in range(min(NSLOT - 1, len(specs))):
                g._issue()

        def _issue(g):
            p, KG, GC = g.specs[g.loaded]
            g.q.append(g.B.gran_load(p, KG, GC))
            g.loaded += 1

        def get(g):
            r = g.q.pop(0)
            if g.loaded < len(g.specs):
                g._issue()
            return r

    def build(s):
        c = s.c
        nc = s.nc
        P = s.P
        T, D, DC, NT, NH = c.T, c.D, c.DC, c.NT, c.NH
        H, HH, QC, KVC, DFC = c.H, c.HH, c.QC, c.KVC, c.DFC
        NCH = T // HCH

        x_ctx = s.dram_in("x_ctx", [T, D])
        x_main = s.dram_in("x_main", [T, D])
        pos_ctx = s.dram_in("pos_ctx", [1, T], I32)
        pos_main = s.dram_in("pos_main", [1, T], I32)
        cst = s.dram_in("cst", [128, 8])
        ident_d = s.dram_in("ident", [128, 128])
        matt_d = s.dram_in("mask_att", [128, 128])
        mhg_d = s.dram_in("mask_hg", [128, 128])
        rst_d = s.dram_in("resetm", [128, T])
        w_in = s.dram_in("w_in", [D, c.NIN])
        w_uq = s.dram_in("w_uq", [c.QL, H * 192])
        w_ukv = s.dram_in("w_ukv", [c.KVL, H * 256])
        w_a = s.dram_in("w_a", [H * 128, D])
        w_b = s.dram_in("w_b", [HH * 128, D])
        w_out = s.dram_in("w_out", [D, D])
        w_gate = s.dram_in("w_gate", [D, c.DFF])
        w_up = s.dram_in("w_up", [D, c.DFF])
        w_down = s.dram_in("w_down", [c.DFF, D])
        lnin_g = s.dram_in("lnin_g", [1, D]); lnin_b = s.dram_in("lnin_b", [1, D])
        ln1_g = s.dram_in("ln1_g", [1, D]); ln1_b = s.dram_in("ln1_b", [1, D])
        ln2_g = s.dram_in("ln2_g", [1, D]); ln2_b = s.dram_in("ln2_b", [1, D])
        qng_d = s.dram_in("qng", [128, QC])
        kvng_d = s.dram_in("kvng", [128, KVC])
        lb_d = s.dram_in("hglb", [128, 2 * HH])
        hgn_d = s.dram_in("hgn", [128, 1])
        gpc_d = s.dram_in("lnin_gpc", [128, DC])
        bpc_d = s.dram_in("lnin_bpc", [128, DC])
        out = nc.dram_tensor("out", [T, D], F32, kind="ExternalOutput").ap()

        hres = s.dram_tmp("hres", [T, D], F32)
        ysc = s.dram_tmp("ysc", [T, D], F32)
        h1res = s.dram_tmp("h1res", [T, D], F32)
        oa_sc = s.dram_tmp("oa_sc", [H, 128, T], BF16)
        ob_sc = s.dram_tmp("ob_sc", [HH, 128, T], BF16)
        mg_sc = s.dram_tmp("mg_sc", [DC, 128, T], BF16)
        act_sc = s.dram_tmp("act_sc", [DFC, 128, T], BF16)
        b_hres = [Buf("hres%d" % i) for i in range(NT)]
        b_ysc = [Buf("ysc%d" % i) for i in range(NT)]
        b_h1res = [Buf("h1res%d" % i) for i in range(NT)]
        b_oasc = [Buf("oasc%d" % i) for i in range(H)]
        b_obsc = [Buf("obsc%d" % i) for i in range(HH)]
        b_mgsc = [Buf("mgsc%d" % i) for i in range(DC)]
        b_actsc = [Buf("actsc%d" % i) for i in range(DFC)]

        s.arena = s.st.enter_context(nc.sbuf_tensor("arena", [128, ARENA_BYTES // 2], BF16))
        s.slot_bufs = [Buf("slot%d" % i) for i in range(NSLOT)]
        ps = [s.st.enter_context(nc.psum_tensor("ps%d" % i, [128, 512], F32))[:, :] for i in range(8)]
        b_ps = [Buf("ps%d" % i) for i in range(8)]

        O_RING = 0
        O_R1 = NSLOT * SLOT_BYTES
        R1_BYTES = DC * T * 2
        O_Z = O_R1 + R1_BYTES
        hT = s.view(O_R1, [128, DC, T], BF16)
        b_r1 = [Buf("r1_%d" % i) for i in range(NT)]

        o = O_Z
        ident = s.view(o, [128, 128], F32); o += 512
        matt = s.view(o, [128, 128], F32); o += 512
        mhg = s.view(o, [128, 128], F32); o += 512
        ones_bf = s.view(o, [128, 128], BF16); o += 256
        identb = s.view(o, [128, 128], BF16); o += 256
        cst_sb = s.view(o, [128, 8], F32); o += 32
        fbias = s.view(o, [128, 1], F32); o += 4
        negpi = s.view(o, [128, 1], F32); o += 4
        epsc = s.view(o, [128, 1], F32); o += 4
        qng = s.view(o, [128, QC], F32); o += 4 * QC
        kvng = s.view(o, [128, KVC], F32); o += 4 * KVC
        lbraw = s.view(o, [128, 2 * HH], F32); o += 8 * HH
        lb = s.view(o, [128, HH], F32); o += 4 * HH
        oml = s.view(o, [128, HH], F32); o += 4 * HH
        hgn = s.view(o, [128, 1], F32); o += 4
        gpc = s.view(o, [128, DC], F32); o += 4 * DC
        bpc = s.view(o, [128, DC], F32); o += 4 * DC
        o = (o + 63) // 64 * 64
        resetm = s.view(o, [128, T], F32); o += 4 * T
        O_P = o
        ckvnT = s.view(o, [128, KVC, 2 * T], BF16); o += KVC * 2 * T * 2
        krope2 = s.view(o, [128, 2 * T], BF16); o += 4 * T
        Sall = s.view(o, [128, HH, 128], F32); o += HH * 512
        O_S = o
        b_const = Buf("const")
        b_ckvn = [Buf("ckvn0"), Buf("ckvn1")]
        b_krope = [Buf("krope0"), Buf("krope1")]
        b_S = [Buf("S%d" % i) for i in range(HH)]
        flag = cst_sb[:, 0:1]
        invf = cst_sb[:, 1:2]
        sgn = cst_sb[:, 3:4]

        s.dmag("sp", [(ident, ident_d), (matt, matt_d), (mhg, mhg_d), (cst_sb, cst), (qng, qng_d),
                      (kvng, kvng_d), (lbraw, lb_d), (hgn, hgn_d), (resetm, rst_d), (gpc, gpc_d), (bpc, bpc_d)],
               b_const, [], [b_const])
        P.op("dve", lambda e: e.memset(ones_bf, 1.0), [], [b_const])
        P.op("dve", lambda e: e.memset(negpi, -math.pi), [], [b_const])
        P.op("dve", lambda e: e.memset(epsc, EPS), [], [b_const])

        def rsqrt_eps(out, in_, reads, writes):
            s.act(out, in_, AF.Sqrt, list(reads) + [b_const], writes, bias=epsc)
            P.op("dve", lambda e: e.reciprocal(out=out, in_=out), writes, writes)
        s.ts(fbias, flag, -1.0, 30000.0, ALU.add, ALU.mult, [b_const], [b_const])
        s.tt(lb, lbraw[:, 0:HH], lbraw[:, HH:2 * HH], ALU.subtract, [b_const], [b_const])
        s.act(lb, lb, AF.Sigmoid, [b_const], [b_const])
        s.ts(oml, lb, -1.0, 1.0, ALU.mult, ALU.add, [b_const], [b_const])

        def ln_stage(src, b_src, g_row, b_row, dst, b_dst, want_T, O_X=None, NB=2, late=None):
            o = O_S if O_X is None else O_X
            xt = [s.view(o + i * 4 * D, [128, D], F32) for i in range(NB)]; o += NB * 4 * D
            if late is None:
                gbc = s.view(o, [128, D], F32); o += 4 * D
                bbc = s.view(o, [128, D], F32); o += 4 * D
            nst = max(1, D // 512)
            stats = [s.view(o + i * nst * 24, [128, nst, 6], F32) for i in range(NB)]; o += NB * nst * 24
            small = [s.view(o + i * 16, [128, 4], F32) for i in range(NB)]; o += NB * 16
            assert o <= ARENA_BYTES, o
            b_xt = [Buf("xt%d" % i) for i in range(NB)]
            b_sm = [Buf("sm%d" % i) for i in range(NB)]
            b_gb = Buf("gb")
            if late is None:
                s.dmag("sp", [(gbc, g_row.broadcast_to([128, D])), (bbc, b_row.broadcast_to([128, D]))], b_gb, [],
                       [b_gb])
            stores = []

            def front(tt_):
                p = tt_ % NB
                X = xt[p]
                rows = slice(tt_ * 128, (tt_ + 1) * 128)
                rd = [b_src[tt_]] if b_src is not None else []
                s.dma1("sp", X, src[rows, :], b_xt[p], rd, [b_xt[p]])
                for k in range(nst):
                    w = min(512, D)
                    P.op("dve", (lambda o_, i_: lambda e: e.bn_stats(out=o_, in_=i_))(
                        stats[p][:, k, :], X[:, k * w:(k + 1) * w]), [b_xt[p]], [b_sm[p]])
                P.op("dve", (lambda o_, i_: lambda e: e.bn_aggr(out=o_, in_=i_))(
                    small[p][:, 0:2], stats[p]), [b_sm[p]], [b_sm[p]])
                rsqrt_eps(small[p][:, 2:3], small[p][:, 1:2], [b_sm[p]], [b_sm[p]])
                s.stt(small[p][:, 3:4], small[p][:, 0:1], -1.0, small[p][:, 2:3], ALU.mult, ALU.mult,
                      [b_sm[p]], [b_sm[p]])
                s.act(X, X, AF.Identity, [b_xt[p], b_sm[p]], [b_xt[p]], bias=small[p][:, 3:4],
                      scale=small[p][:, 2:3])

            def mid(tt_):
                p = tt_ % NB
                X = xt[p]
                rows = slice(tt_ * 128, (tt_ + 1) * 128)
                if late is None:
                    s.tt(X, X, gbc, ALU.mult, [b_xt[p], b_gb], [b_xt[p]])
                    s.tt(X, X, bbc, ALU.add, [b_xt[p], b_gb], [b_xt[p]])
                if dst is not None:
                    stores.append(s.dma1("act", dst[rows, :], X, b_xt[p], [b_xt[p]], [b_dst[tt_]]))

            def back(tt_):
                p = tt_ % NB
                X = xt[p]
                if want_T:
                    for g4 in range(DC // 4):
                        bk = (tt_ * (DC // 4) + g4) % 8
                        for j in range(4):
                            cc = g4 * 4 + j
                            s.tr(ps[bk][:, j * 128:(j + 1) * 128], X[:, cc * 128:(cc + 1) * 128], ident,
                                 [b_xt[p], b_const], [b_ps[bk]])
                        if late is None:
                            s.copy(hT[:, g4 * 4:(g4 + 1) * 4, tt_ * 128:(tt_ + 1) * 128],
                                   ps[bk].rearrange("p (a b) -> p a b", a=4), [b_ps[bk]], [b_r1[tt_]])
                        else:
                            gpc_, bpc_ = late
                            for j in range(4):
                                cc = g4 * 4 + j
                                o_ = hT[:, cc, tt_ * 128:(tt_ + 1) * 128]
                                i_ = ps[bk][:, j * 128:(j + 1) * 128]
                                if j % 2 == 0:
                                    s.act(o_, i_, AF.Identity, [b_ps[bk], b_const], [b_r1[tt_]],
                                          bias=bpc_[:, cc:cc + 1], scale=gpc_[:, cc:cc + 1])
                                else:
                                    s.ts(o_, i_, gpc_[:, cc:cc + 1], bpc_[:, cc:cc + 1], ALU.mult, ALU.add,
                                         [b_ps[bk], b_const], [b_r1[tt_]])

            if NB >= 3:
                for i in range(NT + 2):
                    if i < NT:
                        front(i)
                    if 0 <= i - 1 < NT:
                        mid(i - 1)
                    if 0 <= i - 2 < NT:
                        back(i - 2)
            else:
                for i in range(NT + 1):
                    if i < NT:
                        front(i)
                    if 0 <= i - 1 < NT:
                        mid(i - 1)
                        back(i - 1)
            return stores

        def fm_acc(gv, gb, a, b_, nk, kg0, KC, banks, actT, actb, t0=0, M=None):
            M = (b_ - a) if M is None else M
            for k in range(nk):
                kk = kg0 + k
                for th, bk in enumerate(banks):
                    s.mm(ps[bk][0:M, :], gv[:, k, a:b_], actT[:, kk, t0 + th * 512: t0 + (th + 1) * 512],
                         kk == 0, kk == KC - 1, [gb] + actb(th), [b_ps[bk]])

        def r1b(th):
            return b_r1[th * 4:(th + 1) * 4]

        def rms_fm(raw, b_raw, ncn, gcol, outT_fn, b_out, nth, sq, b_sq, rstd, b_rstd, n):
            for th in range(nth):
                tsl = slice(th * 512, (th + 1) * 512)
                for ch in range(ncn):
                    s.act(sq[ch % 2], raw[:, ch, tsl], AF.Square, [b_raw], [b_sq[ch % 2]], scale=float(n) ** -0.5)
                    s.mm(ps[7][:, :], ones_bf, sq[ch % 2], ch == 0, ch == ncn - 1, [b_sq[ch % 2], b_const], [b_ps[7]])
                rsqrt_eps(rstd, ps[7][:, :], [b_ps[7]], [b_rstd])
                for ch in range(ncn):
                    s.stt(outT_fn(ch, th), raw[:, ch, tsl], gcol[:, ch:ch + 1], rstd, ALU.mult, ALU.mult,
                          [b_raw, b_rstd, b_const], [b_out])

        def rope_tables(pos_d, cos4, ssin4, b_tab, scratch_i, scratch_f, b_scr):
            s.dma1("sp", scratch_i, pos_d.broadcast_to([128, T]), b_scr, [], [b_scr])
            P.op("dve", lambda e: e.tensor_copy(out=scratch_f, in_=scratch_i), [b_scr], [b_scr])
            s.ts(scratch_f, scratch_f, invf, None, ALU.mult, None, [b_scr, b_const], [b_scr])

            def sin_of(dst, shift):
                s.ts(dst, scratch_f, shift, 1.0 / (2 * math.pi), ALU.add, ALU.mult, [b_scr], [b_tab])
                P.op("dve", lambda e: e.tensor_copy(out=scratch_i, in_=dst), [b_tab, b_scr], [b_scr2])
                P.op("dve", lambda e: e.tensor_copy(out=dst, in_=scratch_i), [b_scr2], [b_tab])
                s.ts(dst, dst, -2 * math.pi, shift, ALU.mult, ALU.add, [b_tab], [b_tab])
                s.tt(dst, dst, scratch_f, ALU.add, [b_tab, b_scr], [b_tab])
                s.ts(dst, dst, -3.14159, 3.14159, ALU.max, ALU.min, [b_tab], [b_tab])
                s.act(dst, dst, AF.Sin, [b_tab], [b_tab])

            b_scr2 = Buf("scr2")
            sin_of(ssin4, 0.0)
            s.ts(ssin4, ssin4, sgn, None, ALU.mult, None, [b_tab, b_const], [b_tab])
            sin_of(cos4, 0.5 * math.pi)

        def proj_small(hf, pos_d, want_q, O_X, cqnT, b_cqn, tq, b_tq):
            o = O_X
            cos4 = s.view(o, [128, T], F32); o += 4 * T
            ssin4 = s.view(o, [128, T], F32); o += 4 * T
            sci = s.view(o, [128, T], I32); o += 4 * T
            scf = s.view(o, [128, T], F32); o += 4 * T
            nraw = max(KVC, QC if want_q else 0)
            raw = s.view(o, [128, nraw, T], F32); o += nraw * T * 4
            sq = [s.view(o + i * 1024, [128, 512], BF16) for i in range(2)]; o += 2048
            rstd = s.view(o, [128, 512], F32); o += 2048
            ta = s.view(o, [128, 512], F32); o += 2048
            b_tab, b_scr, b_raw, b_rstd, b_ta = Buf("tab"), Buf("scr"), Buf("raw"), Buf("rstd"), Buf("ta")
            b_sq = [Buf("sq0"), Buf("sq1")]
            rope_tables(pos_d, cos4, ssin4, b_tab, sci, scf, b_scr)
            if want_q:
                P.op("dve", lambda e: e.tensor_copy(out=tq[0:64, :], in_=cos4[0:64, :]), [b_tab], [b_tq])
                P.op("dve", lambda e: e.tensor_copy(out=tq[64:128, :], in_=ssin4[64:128, :]), [b_tab], [b_tq])
            specs = []
            if want_q:
                for pnl in range(c.QL // 256):
                    for (k0, nk) in kch(DC, 16):
                        specs.append(([(0, w_in, k0, nk, c.o_cq + pnl * 256, 256)], 16, 256))
            for pnl in range(c.KVL // 128):
                specs.append(([(0, w_in, 0, DC, c.o_ckv + pnl * 128, 128)], DC, 128))
            kr = c.o_kr
            for (k0, nk) in kch(DC, 16):
                specs.append(([(0, w_in, k0, nk, kr, 64), (64, w_in, k0, nk, kr, 64),
                               (128, w_in, k0, nk, kr + 32, 32), (160, w_in, k0, nk, kr, 32),
                               (192, w_in, k0, nk, kr + 32, 32), (224, w_in, k0, nk, kr, 32)], 16, 256))
            gs = Builder.GStream(s, specs)
            if want_q:
                for pnl in range(c.QL // 256):
                    for (k0, nk) in kch(DC, 16):
                        gv, gb = gs.get()
                        for ct in range(2):
                            fm_acc(gv, gb, ct * 128, (ct + 1) * 128, nk, k0, DC,
                                   [ct * NH + th for th in range(NH)], hT, r1b)
                    for ct in range(2):
                        for th in range(NH):
                            s.copy(raw[:, pnl * 2 + ct, th * 512:(th + 1) * 512], ps[ct * NH + th][:, :],
                                   [b_ps[ct * NH + th]], [b_raw])
                rms_fm(raw, b_raw, QC, qng, lambda ch, th: cqnT[:, ch, th * 512:(th + 1) * 512], b_cqn, NH,
                       sq, b_sq, rstd, b_rstd, c.QL)
            for pnl in range(c.KVL // 128):
                gv, gb = gs.get()
                fm_acc(gv, gb, 0, 128, DC, 0, DC, [4 + th for th in range(NH)], hT, r1b)
                for th in range(NH):
                    s.copy(raw[:, pnl, th * 512:(th + 1) * 512], ps[4 + th][:, :], [b_ps[4 + th]], [b_raw])
            rms_fm(raw, b_raw, KVC, kvng,
                   lambda ch, th: ckvnT[:, ch, hf * T + th * 512: hf * T + (th + 1) * 512], b_ckvn[hf], NH,
                   sq, b_sq, rstd, b_rstd, c.KVL)
            for (k0, nk) in kch(DC, 16):
                gv, gb = gs.get()
                fm_acc(gv, gb, 0, 128, nk, k0, DC, [th for th in range(NH)], hT, r1b)
                fm_acc(gv, gb, 128, 256, nk, k0, DC, [2 + th for th in range(NH)], hT, r1b)
            for th in range(NH):
                tsl = slice(th * 512, (th + 1) * 512)
                s.tt(ta, ps[th][:, :], cos4[:, tsl], ALU.mult, [b_ps[th], b_tab], [b_ta])
                s.tt(rstd, ps[2 + th][:, :], ssin4[:, tsl], ALU.mult, [b_ps[2 + th], b_tab], [b_rstd])
                s.tt(krope2[:, hf * T + th * 512: hf * T + (th + 1) * 512], ta, rstd, ALU.add,
                     [b_ta, b_rstd], [b_krope[hf]])

        def hgrn(hf, main, O_X):
            o = O_X
            Fb = [s.view(o + i * 4 * T, [128, T], F32) for i in range(4)]; o += 16 * T
            G = [s.view(o + i * 4 * T, [128, T], F32) for i in range(2)]; o += 8 * T
            qtT2 = [s.view(o + i * 2 * T, [128, T], BF16) for i in range(2)]; o += 4 * T
            vsb2 = [s.view(o + i * 2 * T, [128, NT, 128], BF16) for i in range(2)]; o += 4 * T
            ktT = s.view(o, [128, T], BF16); o += 2 * T
            khT = s.view(o, [128, T], BF16); o += 2 * T
            khat = s.view(o, [128, NT, 128], BF16); o += 2 * T
            ATs = s.view(o, [128, NT, 128], BF16); o += 2 * T
            obs = s.view(o, [128, T], BF16); o += 2 * T
            eB2 = [s.view(o + i * 4 * NCH, [128, NCH], F32) for i in range(2)]; o += 8 * NCH
            Sf = s.view(o, [128, NCH + 1, 128], F32); o += (NCH + 1) * 512
            Sbl = s.view(o, [128, NCH, 128], BF16); o += NCH * 256
            sq = s.view(o, [128, 512], BF16); o += 1024
            rstd = s.view(o, [128, 512], F32); o += 2048
            tmp = s.view(o, [128, 512], F32); o += 2048
            assert o <= ARENA_BYTES, o
            bF = [Buf("F%d" % i) for i in range(4)]
            b_G = [Buf("G0"), Buf("G1")]
            b_qt2 = [Buf("qt0"), Buf("qt1")]
            b_v2 = [Buf("v0"), Buf("v1")]
            b_eB2 = [Buf("eB0"), Buf("eB1")]
            b_kt, b_kh, b_khat, b_AT, b_obs, b_Sb, b_Sf, b_sq, b_rstd, b_tmp = [Buf(n) for n in (
                "kt", "kh", "khat", "AT", "obs", "Sb", "Sf", "sq", "rstd", "tmp")]
            F0, F1, F2, F3 = Fb
            specs = []
            for h in range(HH):
                cols = [c.o_hf + h * 128] + ([c.o_hq + h * 128] if main else [])
                for (k0, nk) in kch(DC, 16):
                    specs.append(([(i * 128, w_in, k0, nk, cc, 128) for i, cc in enumerate(cols)], 16, 256))
                specs.append(([(0, w_in, 0, DC, c.o_hi + h * 128, 128)], DC, 128))
                if main:
                    specs.append(([(0, w_in, 0, DC, c.o_hg + h * 128, 128)], DC, 128))
            gs = Builder.GStream(s, specs)
            F3c = F3.rearrange("p (c l) -> p c l", l=HCH)
            F2c = F2.rearrange("p (c l) -> p c l", l=HCH)
            nU = NCH - 1 if main else NCH

            def ubank(ck):
                return 4 + (ck % 2) + 2 * ((ck // 2) // 4), ((ck // 2) % 4) * 128

            def ph_P1(h):
                for (k0, nk) in kch(DC, 16):
                    gv, gb = gs.get()
                    fm_acc(gv, gb, 0, 128, nk, k0, DC, [th for th in range(NH)], hT, r1b)
                    if main:
                        fm_acc(gv, gb, 128, 256, nk, k0, DC, [2 + th for th in range(NH)], hT, r1b)
                for th in range(NH):
                    tsl = slice(th * 512, (th + 1) * 512)
                    s.act(F1[:, tsl], ps[th][:, :], AF.Sigmoid, [b_ps[th]], [bF[1]])
                    if main:
                        s.act(F0[:, tsl], ps[2 + th][:, :], AF.Silu, [b_ps[2 + th]], [bF[0]])

            def ph_P23(h):
                vsb = vsb2[h % 2]
                vb0 = 0 if main else 2
                gv, gb = gs.get()
                for t_ in range(NT):
                    bk = vb0 + t_ // 4
                    for k in range(DC):
                        s.mm(ps[bk][:, (t_ % 4) * 128:(t_ % 4 + 1) * 128], hT[:, k, t_ * 128:(t_ + 1) * 128],
                             gv[:, k, 0:128], k == 0, k == DC - 1, [gb, b_r1[t_]], [b_ps[bk]])
                for g4 in range(NT // 4):
                    s.copy(vsb[:, g4 * 4:(g4 + 1) * 4, :], ps[vb0 + g4].rearrange("p (a b) -> p a b", a=4),
                           [b_ps[vb0 + g4]], [b_v2[h % 2]])
                if main:
                    gv, gb = gs.get()
                    fm_acc(gv, gb, 0, 128, DC, 0, DC, [2 + th for th in range(NH)], hT, r1b)
                    for th in range(NH):
                        s.act(G[h % 2][:, th * 512:(th + 1) * 512], ps[2 + th][:, :], AF.Silu, [b_ps[2 + th]],
                              [b_G[h % 2]])

            def ph_E(h):
                qtT, b_qt = qtT2[h % 2], b_qt2[h % 2]
                eB, b_eB = eB2[h % 2], b_eB2[h % 2]
                s.ts(F1, F1, oml[:, h:h + 1], lb[:, h:h + 1], ALU.mult, ALU.add, [bF[1], b_const], [bF[1]])
                s.act(F2, F1, AF.Ln, [bF[1]], [bF[2]])
                s.ts(F1, F1, -1.0, 1.0, ALU.mult, ALU.add, [bF[1]], [bF[1]])
                P.op("dve", lambda e: e.tensor_tensor_scan(out=F3, data0=resetm, data1=F2, initial=0.0,
                                                           op0=ALU.mult, op1=ALU.add),
                     [bF[2], b_const], [bF[3]])
                if main:
                    s.act(F2, F3, AF.Exp, [bF[3]], [bF[2]])
                    s.tt(qtT, F0, F2, ALU.mult, [bF[0], bF[2]], [b_qt])
                    s.act(F2, F3, AF.Exp, [bF[3]], [bF[2]], scale=-1.0)
                    s.tt(ktT, F1, F2, ALU.mult, [bF[1], bF[2]], [b_kt])
                s.act(eB, F3c[:, :, HCH - 1], AF.Exp, [bF[3]], [b_eB])
                s.tt(F2c, F3c, F3c[:, :, HCH - 1:HCH].to_broadcast([128, NCH, HCH]), ALU.subtract,
                     [bF[3]], [bF[2]])
                s.act(F2, F2, AF.Exp, [bF[2]], [bF[2]], scale=-1.0)
                s.tt(khT, F1, F2, ALU.mult, [bF[1], bF[2]], [b_kh])

            def ph_R1(h):
                qtT, b_qt = qtT2[h % 2], b_qt2[h % 2]
                vsb, b_v = vsb2[h % 2], b_v2[h % 2]
                eB, b_eB = eB2[h % 2], b_eB2[h % 2]
                for g4 in range(NT // 4):
                    bk = 4 + g4
                    pv = ps[bk][:, 0:256].bitcast(BF16)
                    for j in range(4):
                        t_ = g4 * 4 + j
                        s.tr(pv[:, j * 128:(j + 1) * 128], khT[:, t_ * 128:(t_ + 1) * 128], identb,
                             [b_kh, b_const], [b_ps[bk]])
                    s.copy(khat[:, g4 * 4:(g4 + 1) * 4, :], pv.rearrange("p (a b) -> p a b", a=4),
                           [b_ps[bk]], [b_khat])
                S = Sall[:, h, :]
                if not main:
                    P.op("dve", (lambda S_: lambda e: e.memset(S_, 0.0))(S), [], [b_S[h]])
                if main:
                    for g4 in range(NT // 4):
                        bk = 6 + g4
                        for j in range(4):
                            t_ = g4 * 4 + j
                            tl = slice(t_ * 128, (t_ + 1) * 128)
                            s.mm(ps[bk][:, j * 128:(j + 1) * 128], ktT[:, tl], qtT[:, tl], True, True,
                                 [b_kt, b_qt], [b_ps[bk]])
                        s.tt(ATs[:, g4 * 4:(g4 + 1) * 4, :], ps[bk].rearrange("p (a b) -> p a b", a=4),
                             mhg.unsqueeze(1).to_broadcast([128, 4, 128]), ALU.mult, [b_ps[bk], b_const], [b_AT])
                for ck in range(nU):
                    t_ = ck // 2
                    pr = slice((ck % 2) * 64, (ck % 2) * 64 + 64)
                    ubk, uc = ubank(ck)
                    s.mm(ps[ubk][:, uc:uc + 128], khat[pr, t_, :], vsb[pr, t_, :], True, True,
                         [b_khat, b_v], [b_ps[ubk]])
                P.op("dve", (lambda S_: lambda e: e.tensor_copy(out=Sf[:, 0, :], in_=S_))(S), [b_S[h]], [b_Sf])
                for ck in range(nU):
                    ubk, uc = ubank(ck)
                    last = (not main and ck == nU - 1)
                    dst_ = S if last else Sf[:, ck + 1, :]
                    wr = [b_S[h]] if last else [b_Sf]
                    s.stt(dst_, Sf[:, ck, :], eB[:, ck:ck + 1], ps[ubk][:, uc:uc + 128],
                          ALU.mult, ALU.add, [b_Sf, b_eB, b_ps[ubk]], wr)
                if main:
                    s.act(Sbl, Sf[:, 0:NCH, :], AF.Copy, [b_Sf], [b_Sb])
                else:
                    s.ts(S, S, flag, None, ALU.mult, None, [b_S[h], b_const], [b_S[h]])

            def ph_R2(h):
                qtT, b_qt = qtT2[h % 2], b_qt2[h % 2]
                vsb, b_v = vsb2[h % 2], b_v2[h % 2]
                for t_ in range(NT):
                    ob_bk = 4 + t_ // 4
                    col0 = (t_ % 4) * 128
                    s.mm(ps[ob_bk][:, col0:col0 + 128], vsb[:, t_, :], ATs[:, t_, :], True, False,
                         [b_v, b_AT], [b_ps[ob_bk]])
                    for cc in range(2):
                        ck = 2 * t_ + cc
                        cs = col0 + cc * 64
                        s.mm(ps[ob_bk][:, cs:cs + 64], Sbl[:, ck, :], qtT[:, ck * 64:(ck + 1) * 64], False,
                             cc == 1, [b_Sb, b_qt], [b_ps[ob_bk]])
                for th in range(NH):
                    tsl = slice(th * 512, (th + 1) * 512)
                    obk = 4 + th
                    s.act(sq, ps[obk][:, :], AF.Square, [b_ps[obk]], [b_sq], scale=128.0 ** -0.5)
                    s.mm(ps[6][:, :], ones_bf, sq, True, True, [b_sq, b_const], [b_ps[6]])
                    rsqrt_eps(rstd, ps[6][:, :], [b_ps[6]], [b_rstd])
                    s.stt(tmp, ps[obk][:, :], hgn[:, 0:1], rstd, ALU.mult, ALU.mult, [b_ps[obk], b_rstd, b_const],
                          [b_tmp])
                    s.tt(obs[:, tsl], tmp, G[h % 2][:, tsl], ALU.mult, [b_tmp, b_G[h % 2]], [b_obs])
                s.dma1("sp", ob_sc[h], obs, b_obs, [b_obs], [b_obsc[h]])

            for i in range(HH + 1):
                if i < HH:
                    ph_P1(i)
                if i >= 1:
                    ph_R1(i - 1)
                if i < HH:
                    ph_P23(i)
                if i >= 1 and main:
                    ph_R2(i - 1)
                if i < HH:
                    ph_E(i)

        def attention(O_X, cqnT, b_cqn, tq, b_tq):
            o = O_X
            qn2 = [s.view(o + i * 2 * T, [128, T], BF16) for i in range(2)]; o += 4 * T
            Rq2 = [s.view(o + i * 2 * T, [128, T], BF16) for i in range(2)]; o += 4 * T
            kT2 = [s.view(o + i * 4 * T, [128, 2 * T], BF16) for i in range(2)]; o += 8 * T
            vv2 = [s.view(o + i * 4 * T, [128, 2 * NT, 128], BF16) for i in range(2)]; o += 8 * T
            pT = [s.view(o + i * 1024, [128, 512], BF16) for i in range(2)]; o += 2048
            rec = s.view(o, [128, 512], F32); o += 2048
            oas = s.view(o, [128, T], BF16); o += 2 * T
            assert o <= ARENA_BYTES, o
            b_rec, b_oas = Buf("rec"), Buf("oas")
            b_qn2, b_Rq2, b_kT2, b_vv2 = [[Buf(n + "0"), Buf(n + "1")] for n in ("qn", "Rq", "kT", "vv")]
            b_pT = [Buf("pT0"), Buf("pT1")]
            scale = float(NOPE + ROPE) ** -0.5
            specs = []
            for h in range(H):
                q0 = h * 192
                specs.append(([(0, w_uq, 0, QC, q0, 128), (128, w_uq, 0, QC, q0 + 128, 64),
                               (192, w_uq, 0, QC, q0 + 160, 32), (224, w_uq, 0, QC, q0 + 128, 32)], QC, 256))
                specs.append(([(0, w_ukv, 0, KVC, h * 256, 256)], KVC, 256))
            gs = Builder.GStream(s, specs)
            NK = 2 * NT
            blk = 0
            def att_proj(h):
                qn, Rq, kT, vv = qn2[h % 2], Rq2[h % 2], kT2[h % 2], vv2[h % 2]
                b_qn, b_Rq, b_kT, b_vv = b_qn2[h % 2], b_Rq2[h % 2], b_kT2[h % 2], b_vv2[h % 2]
                gv, gb = gs.get()
                for th in range(NH):
                    for k in range(QC):
                        s.mm(ps[th][:, :], gv[:, k, 0:128], cqnT[:, k, th * 512:(th + 1) * 512], k == 0, k == QC - 1,
                             [gb, b_cqn], [b_ps[th]])
                    for k in range(QC):
                        s.mm(ps[2 + th][:, :], gv[:, k, 128:256], cqnT[:, k, th * 512:(th + 1) * 512], k == 0,
                             k == QC - 1, [gb, b_cqn], [b_ps[2 + th]])
                for th in range(NH):
                    tsl = slice(th * 512, (th + 1) * 512)
                    s.copy(qn[:, tsl], ps[th][:, :], [b_ps[th]], [b_qn])
                    s.tt(Rq[:, tsl], ps[2 + th][:, :], tq[:, tsl], ALU.mult, [b_ps[2 + th], b_tq], [b_Rq])
                gv, gb = gs.get()
                for kg in range(2 * NH):
                    bk = kg % 4
                    for k in range(KVC):
                        s.mm(ps[bk][:, :], gv[:, k, 0:128], ckvnT[:, k, kg * 512:(kg + 1) * 512], k == 0,
                             k == KVC - 1, [gb] + b_ckvn, [b_ps[bk]])
                    s.copy(kT[:, kg * 512:(kg + 1) * 512], ps[bk][:, :], [b_ps[bk]], [b_kT])
                for g4 in range(NK // 4):
                    bk = g4 % 4
                    for j in range(4):
                        kt = g4 * 4 + j
                        for k in range(KVC):
                            s.mm(ps[bk][:, j * 128:(j + 1) * 128], ckvnT[:, k, kt * 128:(kt + 1) * 128],
                                 gv[:, k, 128:256], k == 0, k == KVC - 1, [gb] + b_ckvn, [b_ps[bk]])
                    s.copy(vv[:, g4 * 4:(g4 + 1) * 4, :], ps[bk].rearrange("p (a b) -> p a b", a=4),
                           [b_ps[bk]], [b_vv])

            def att_blocks(h):
                nonlocal blk
                qn, Rq, kT, vv = qn2[h % 2], Rq2[h % 2], kT2[h % 2], vv2[h % 2]
                b_qn, b_Rq, b_kT, b_vv = b_qn2[h % 2], b_Rq2[h % 2], b_kT2[h % 2], b_vv2[h % 2]
                for g in range(NH):
                    last_kt = NT + 4 * g + 3
                    bls = []
                    for kt in range(last_kt + 1):
                        is_ctx = kt < NT
                        off = 0 if is_ctx else max(0, (kt - NT) * 128 - g * 512)
                        bls.append((kt, is_ctx, off, 4 + blk % 2, blk % 2))
                        blk += 1

                    def emit_S(bl, g=g):
                        kt, is_ctx, off, sb, pb = bl
                        n = 512 - off
                        q0 = g * 512 + off
                        s.mm(ps[sb][:, off:512], kT[:, kt * 128:(kt + 1) * 128], qn[:, q0:q0 + n], True, False,
                             [b_kT, b_qn], [b_ps[sb]])
                        s.mm(ps[sb][:, off:512], krope2[:, kt * 128:(kt + 1) * 128], Rq[:, q0:q0 + n], False, True,
                             b_krope + [b_Rq], [b_ps[sb]])

                    def emit_exp(bl, g=g):
                        kt, is_ctx, off, sb, pb = bl
                        if is_ctx:
                            s.act(pT[pb][:, off:512], ps[sb][:, off:512], AF.Exp, [b_ps[sb], b_const], [b_pT[pb]],
                                  bias=fbias, scale=scale)
                        else:
                            s.act(pT[pb][:, off:512], ps[sb][:, off:512], AF.Exp, [b_ps[sb]], [b_pT[pb]], scale=scale)
                            if (kt - NT) * 128 >= g * 512:
                                s.tt(pT[pb][:, off:off + 128], pT[pb][:, off:off + 128], matt, ALU.mult,
                                     [b_pT[pb], b_const], [b_pT[pb]])

                    def emit_PV(bl, first, last):
                        kt, is_ctx, off, sb, pb = bl
                        s.mm(ps[6][:, off:512], vv[:, kt, :], pT[pb][:, off:512], first, last,
                             [b_vv, b_pT[pb]], [b_ps[6]])
                        s.mm(ps[7][:, off:512], ones_bf, pT[pb][:, off:512], first, last,
                             [b_const, b_pT[pb]], [b_ps[7]])

                    emit_S(bls[0])
                    for i, bl in enumerate(bls):
                        if i + 1 < len(bls):
                            emit_S(bls[i + 1])
                        emit_exp(bl)
                        emit_PV(bl, i == 0, i == len(bls) - 1)
                    P.op("dve", lambda e: e.reciprocal(out=rec, in_=ps[7][:, :]), [b_ps[7]], [b_rec])
                    s.tt(oas[:, g * 512:(g + 1) * 512], ps[6][:, :], rec, ALU.mult, [b_ps[6], b_rec], [b_oas])
                s.dma1("sp", oa_sc[h], oas, b_oas, [b_oas], [b_oasc[h]])

            att_proj(0)
            for h in range(H):
                if h + 1 < H:
                    att_proj(h + 1)
                att_blocks(h)

        def merge(O_X):
            o = O_X
            HC, HHC = H, HH
            oaT = s.view(o, [128, HC, T], BF16); o += HC * T * 2
            obT = s.view(o, [128, HHC, T], BF16); o += HHC * T * 2
            m1 = s.view(o, [128, 2, T], F32); o += 8 * T
            sg = [s.view(o + i * 2048, [128, 512], F32) for i in range(2)]; o += 4096
            t2 = [s.view(o + i * 2048, [128, 512], F32) for i in range(2)]; o += 4096
            mst = [s.view(o + i * 2 * T, [128, T], BF16) for i in range(2)]; o += 4 * T
            assert o <= ARENA_BYTES, o
            b_oa, b_ob, b_m1 = Buf("oaT"), Buf("obT"), [Buf("m1a"), Buf("m1b")]
            b_sg = [Buf("sg0"), Buf("sg1")]
            b_t2 = [Buf("t20"), Buf("t21")]
            b_mst = [Buf("mst0"), Buf("mst1")]
            s.dmag("sp", [(oaT[:, h, :], oa_sc[h]) for h in range(HC)], b_oa, b_oasc, [b_oa])
            s.dmag("sp", [(obT[:, h, :], ob_sc[h]) for h in range(HHC)], b_ob, b_obsc, [b_ob])
            specs = []
            npan = D // 256
            for pn in range(npan):
                for (Wm, KCm, wgc, goff) in ((w_a, HC, None, c.o_ga), (w_b, HHC, None, c.o_gb)):
                    for (k0, nk) in kch(KCm, 16):
                        specs.append(([(0, Wm, k0, nk, pn * 256, 256)], 16, 256))
                    for (k0, nk) in kch(DC, 16):
                        specs.append(([(0, w_in, k0, nk, goff + pn * 256, 256)], 16, 256))
            gs = Builder.GStream(s, specs)
            ui = 0
            for pn in range(npan):
                for step, (srcT, b_src, KCm) in enumerate(((oaT, b_oa, HC), (obT, b_ob, HHC))):
                    def bank(ct, which, th):
                        return ct * 4 + which * 2 + th
                    for (k0, nk) in kch(KCm, 16):
                        gv, gb = gs.get()
                        for ct in range(2):
                            fm_acc(gv, gb, ct * 128, (ct + 1) * 128, nk, k0, KCm,
                                   [bank(ct, 0, th) for th in range(NH)], srcT, (lambda bb: (lambda th: [bb]))(b_src))
                    for (k0, nk) in kch(DC, 16):
                        gv, gb = gs.get()
                        for ct in range(2):
                            fm_acc(gv, gb, ct * 128, (ct + 1) * 128, nk, k0, DC,
                                   [bank(ct, 1, th) for th in range(NH)], hT, r1b)
                    for ct in range(2):
                        for th in range(NH):
                            tsl = slice(th * 512, (th + 1) * 512)
                            u = ui % 2
                            ui += 1
                            s.act(sg[u], ps[bank(ct, 1, th)][:, :], AF.Sigmoid, [b_ps[bank(ct, 1, th)]], [b_sg[u]])
                            if step == 0:
                                s.tt(m1[:, ct, tsl], ps[bank(ct, 0, th)][:, :], sg[u], ALU.mult,
                                     [b_ps[bank(ct, 0, th)], b_sg[u]], [b_m1[ct]])
                            else:
                                s.tt(t2[u], ps[bank(ct, 0, th)][:, :], sg[u], ALU.mult,
                                     [b_ps[bank(ct, 0, th)], b_sg[u]], [b_t2[u]])
                                s.tt(mst[ct][:, tsl], t2[u], m1[:, ct, tsl], ALU.add, [b_t2[u], b_m1[ct]],
                                     [b_mst[ct]])
                        if step == 1:
                            s.dma1("sp", mg_sc[pn * 2 + ct], mst[ct], b_mst[ct], [b_mst[ct]], [b_mgsc[pn * 2 + ct]])

        def gemm_tm(W, KC, actT, actb, tiles, row0, resid, b_res, dst, b_dst, O_X, sets, kbufs=None, aff=None):
            ntl = len(tiles)
            rs = [[s.view(O_X + (p * ntl + j) * 2048, [128, 512], F32) for j in range(ntl)] for p in range(2)]
            key = ("rs", O_X, ntl)
            if key not in s.bufcache:
                s.bufcache[key] = [[Buf("rs%d_%d" % (p, j)) for j in range(ntl)] for p in range(2)]
            b_rs = s.bufcache[key]
            ncg = D // 512
            KG = 8
            ngr = (KC + KG - 1) // KG
            specs = []
            for cg in range(ncg):
                for gi in range(ngr):
                    nk = min(KG, KC - gi * KG)
                    specs.append(([(0, W, gi * KG, nk, cg * 512, 512)], KG, 512))
            gs = Builder.GStream(s, specs)
            stores = []
            for cg in range(ncg):
                p = cg % 2
                csl = slice(cg * 512, (cg + 1) * 512)
                for j, t_ in enumerate(tiles):
                    r0 = row0 + j * 128
                    s.dma1("sp", rs[p][j], resid[r0:r0 + 128, csl], b_rs[p][j], [b_res[(r0 // 128)]], [b_rs[p][j]])
                    if aff is not None:
                        ga_, ba_, b_aff = aff
                        s.tt(rs[p][j], rs[p][j], ga_[:, csl], ALU.mult, [b_rs[p][j], b_aff], [b_rs[p][j]])
                        s.tt(rs[p][j], rs[p][j], ba_[:, csl], ALU.add, [b_rs[p][j], b_aff], [b_rs[p][j]])
                for gi in range(ngr):
                    nk = min(KG, KC - gi * KG)
                    gv, gb = gs.get()
                    for k in range(nk):
                        kk = gi * KG + k
                        for j, t_ in enumerate(tiles):
                            bk = (p * ntl + j) if sets == 2 else j
                            ab = [kbufs[kk // 8]] if kbufs is not None else actb(t_)
                            s.mm(ps[bk][:, :], actT[:, kk, t_ * 128:(t_ + 1) * 128], gv[:, k, 0:512], kk == 0,
                                 kk == KC - 1, [gb] + ab, [b_ps[bk]])
                for j, t_ in enumerate(tiles):
                    bk = (p * ntl + j) if sets == 2 else j
                    r0 = row0 + j * 128
                    s.stt(rs[p][j], rs[p][j], ALPHA, ps[bk][:, :], ALU.mult, ALU.add, [b_rs[p][j], b_ps[bk]],
                          [b_rs[p][j]])
                    stores.append(s.dma1("act", dst[r0:r0 + 128, csl], rs[p][j], b_rs[p][j], [b_rs[p][j]],
                                         [b_dst[r0 // 128]]))
            return stores

        def ffn1(O_X):
            o = O_X
            sl = [s.view(o + i * 2048, [128, 512], F32) for i in range(2)]; o += 4096
            ast = [s.view(o + i * 2 * T, [128, T], BF16) for i in range(2)]; o += 4 * T
            b_sl = [Buf("sl0"), Buf("sl1")]
            b_ast = [Buf("ast0"), Buf("ast1")]
            npan = c.DFF // 256
            specs = []
            for pn in range(npan):
                for Wm in (w_gate, w_up):
                    for (k0, nk) in kch(DC, 16):
                        specs.append(([(0, Wm, k0, nk, pn * 256, 256)], 16, 256))
            gs = Builder.GStream(s, specs)
            ui = 0
            for pn in range(npan):
                def bank(ct, which, th):
                    return ct * 4 + which * 2 + th
                for which in range(2):
                    for (k0, nk) in kch(DC, 16):
                        gv, gb = gs.get()
                        for ct in range(2):
                            fm_acc(gv, gb, ct * 128, (ct + 1) * 128, nk, k0, DC,
                                   [bank(ct, which, th) for th in range(NH)], hT, r1b)
                for ct in range(2):
                    a = (pn * 2 + ct) % 2
                    for th in range(NH):
                        tsl = slice(th * 512, (th + 1) * 512)
                        u = ui % 2
                        ui += 1
                        s.act(sl[u], ps[bank(ct, 0, th)][:, :], AF.Silu, [b_ps[bank(ct, 0, th)]], [b_sl[u]])
                        s.tt(ast[a][:, tsl], ps[bank(ct, 1, th)][:, :], sl[u], ALU.mult,
                             [b_ps[bank(ct, 1, th)], b_sl[u]], [b_ast[a]])
                    s.dma1("sp", act_sc[pn * 2 + ct], ast[a], b_ast[a], [b_ast[a]], [b_actsc[pn * 2 + ct]])

        P.op("dve", lambda e: e.tensor_copy(out=identb, in_=ident), [b_const], [b_const])

        LNB = 4 if O_P + 6 * 4 * D + 4096 <= ARENA_BYTES else 2
        ln_stage(x_ctx, None, lnin_g, lnin_b, None, None, True, O_P, LNB, late=(gpc, bpc))
        P.barrier()
        proj_small(0, pos_ctx, False, O_S, None, None, None, None)
        P.barrier()
        hgrn(0, False, O_S)
        P.barrier()
        ln_stage(x_main, None, lnin_g, lnin_b, hres, b_hres, True, O_S, 4, late=(gpc, bpc))
        P.barrier()
        cqnT = s.view(O_S, [128, QC, T], BF16)
        tq = s.view(O_S + QC * T * 2, [128, T], F32)
        b_cqn, b_tq = Buf("cqn"), Buf("tq")
        O_M = O_S + QC * T * 2 + 4 * T
        proj_small(1, pos_main, True, O_M, cqnT, b_cqn, tq, b_tq)
        P.barrier()
        attention(O_M, cqnT, b_cqn, tq, b_tq)
        P.barrier()
        hgrn(1, True, O_S)
        P.barrier()
        merge(O_P)
        P.barrier()
        s.dmag("sp", [(hT[:, kc, :], mg_sc[kc]) for kc in range(DC)], b_r1[0], b_mgsc, b_r1)
        O_A = O_S + 2 * NT * 2048
        gaf = s.view(O_A, [128, D], F32)
        baf = s.view(O_A + 4 * D, [128, D], F32)
        assert O_A + 8 * D <= ARENA_BYTES
        b_aff = Buf("aff")
        s.dmag("sp", [(gaf, lnin_g.broadcast_to([128, D])), (baf, lnin_b.broadcast_to([128, D]))], b_aff, [], [b_aff])
        gemm_tm(w_out, DC, hT, lambda t_: [b_r1[t_]], list(range(NT)), 0, hres, b_hres, ysc, b_ysc, O_S, 1,
                aff=(gaf, baf, b_aff))
        P.barrier()
        ln_stage(ysc, b_ysc, ln1_g, ln1_b, h1res, b_h1res, True, O_P, LNB)
        P.barrier()
        ffn1(O_S)
        P.barrier()
        actT = s.view(O_P, [128, DFC, 512], BF16)
        assert O_P + DFC * 1024 <= ARENA_BYTES
        kgs = kch(DFC, 8)
        b_actT = [Buf("actT%d" % i) for i in range(len(kgs))]
        fin = []
        for th in range(NH):
            for gi, (k0, nk) in enumerate(kgs):
                s.dmag("sp", [(actT[:, kc, :], act_sc[kc][:, th * 512:(th + 1) * 512]) for kc in range(k0, k0 + nk)],
                       b_actT[gi], b_actsc[k0:k0 + nk], [b_actT[gi]])
            gemm_tm(w_down, DFC, actT, None, [0, 1, 2, 3], th * 512, h1res, b_h1res, ysc, b_ysc,
                    O_R1 if R1_BYTES >= 16384 else O_P + DFC * 1024, 2, kbufs=b_actT)
        P.barrier()
        fin = ln_stage(ysc, b_ysc, ln2_g, ln2_b, out, [Buf("out%d" % i) for i in range(NT)], False, O_P, LNB)
        P.wait_all("sp", fin)
        print("kernel build: ops=%d dma_sems=%d" % (P.nops, P.nsem))
        P.emit_all()
        s.st.close()
        return nc


def _pc(v, nch):
    return np.ascontiguousarray(np.asarray(v, np.float32).reshape(nch, 128).T)


_CACHE = {}


def make_inputs(cfg, x, positions, ln_in_g, ln_in_b, w_in, q_norm_g, w_uq, kv_norm_g, w_ukv, hg_lb, hg_norm_g,
                w_branch_a, w_branch_b, w_out, ln1_g, ln1_b, w_gate, w_up, w_down, ln2_g, ln2_b):
    T, D = cfg.T, cfg.D
    f32 = lambda a: np.ascontiguousarray(np.asarray(a, np.float32))
    row = lambda a: f32(a).reshape(1, -1)
    p = np.arange(128)
    invf = (10000.0 ** (-(np.arange(32, dtype=np.float32)) / 32.0)).astype(np.float32)[p % 32]
    sgn = np.where((p % 64) < 32, -1.0, 1.0).astype(np.float32)
    kk = np.arange(128)[:, None]
    qq = np.arange(128)[None, :]
    mask_att = np.where((kk >= 64) & (qq < 64), 0.0, 1.0).astype(np.float32)
    mask_hg = np.where((kk // HCH == qq // HCH) & (kk <= qq), 1.0, 0.0).astype(np.float32)
    resetm = np.where(np.arange(T) % HCH == 0, 0.0, 1.0).astype(np.float32)[None, :].repeat(128, 0)
    lb2 = np.asarray(hg_lb, np.float32)
    hglb = np.concatenate([_pc(lb2[0], cfg.HH), _pc(lb2[1], cfg.HH)], axis=1)
    shared = {
        "ident": np.eye(128, dtype=np.float32), "mask_att": mask_att, "mask_hg": mask_hg,
        "resetm": np.ascontiguousarray(resetm),
        "w_in": f32(w_in[0]), "w_uq": f32(w_uq[0]), "w_ukv": f32(w_ukv[0]), "w_a": f32(w_branch_a[0]),
        "w_b": f32(w_branch_b[0]), "w_out": f32(w_out[0]), "w_gate": f32(w_gate[0]), "w_up": f32(w_up[0]),
        "w_down": f32(w_down[0]),
        "lnin_g": row(ln_in_g), "lnin_b": row(ln_in_b), "ln1_g": row(ln1_g[0]), "ln1_b": row(ln1_b[0]),
        "ln2_g": row(ln2_g[0]), "ln2_b": row(ln2_b[0]),
        "qng": _pc(q_norm_g[0], cfg.QC), "kvng": _pc(kv_norm_g[0], cfg.KVC), "hglb": hglb,
        "hgn": _pc(hg_norm_g[0], 1),
        "lnin_gpc": _pc(ln_in_g, cfg.DC), "lnin_bpc": _pc(ln_in_b, cfg.DC),
    }
    x = np.asarray(x, np.float32)
    positions = np.asarray(positions, np.int32)
    maps = []
    B = x.shape[0]
    for core in range(2 * B):
        b, half = core // 2, core % 2
        cst = np.zeros((128, 8), np.float32)
        cst[:, 0] = float(half)
        cst[:, 1] = invf
        cst[:, 3] = sgn
        m = dict(shared)
        m["x_ctx"] = np.ascontiguousarray(x[b, 0:T])
        m["x_main"] = np.ascontiguousarray(x[b, half * T:(half + 1) * T])
        m["pos_ctx"] = np.ascontiguousarray(positions[b, 0:T]).reshape(1, T)
        m["pos_main"] = np.ascontiguousarray(positions[b, half * T:(half + 1) * T]).reshape(1, T)
        m["cst"] = cst
        maps.append(m)
    return maps


def cfg_from_inputs(x, q_norm_g, kv_norm_g, w_uq, hg_lb, w_gate):
    B, SEQ, D = x.shape
    return Cfg(SEQ // 2, D, q_norm_g.shape[1], kv_norm_g.shape[1], w_uq.shape[2] // 192, hg_lb.shape[1] // 128,
               w_gate.shape[2])


def kernel(x, positions, ln_in_g, ln_in_b, w_in, q_norm_g, w_uq, kv_norm_g, w_ukv, hg_lb, hg_norm_g,
           w_branch_a, w_branch_b, w_out, ln1_g, ln1_b, w_gate, w_up, w_down, ln2_g, ln2_b, _debug=()):
    x = np.asarray(x)
    cfg = cfg_from_inputs(x, np.asarray(q_norm_g), np.asarray(kv_norm_g), np.asarray(w_uq), np.asarray(hg_lb),
                          np.asarray(w_gate))
    B, SEQ, D = x.shape
    assert B * 2 == 8
    nc = Builder(cfg, debug=_debug).build()
    maps = make_inputs(cfg, x, positions, ln_in_g, ln_in_b, w_in, q_norm_g, w_uq, kv_norm_g, w_ukv, hg_lb,
                       hg_norm_g, w_branch_a, w_branch_b, w_out, ln1_g, ln1_b, w_gate, w_up, w_down, ln2_g, ln2_b)
    res = run_bass_kernel_spmd(nc, maps, core_ids=list(range(8)))
    T = cfg.T
    outp = np.empty((B, SEQ, D), np.float32)
    for core in range(8):
        b, half = core // 2, core % 2
        outp[b, half * T:(half + 1) * T] = res.results[core]["out"]
    if _debug:
        return outp, res.results
    return outp
```

```python
import math
import numpy as np
from contextlib import ExitStack
import concourse.bass as bass
import concourse.mybir as mybir
from concourse.bass_utils import run_bass_kernel_spmd

F32 = mybir.dt.float32
BF16 = mybir.dt.bfloat16
I32 = mybir.dt.int32
ALU = mybir.AluOpType
AF = mybir.ActivationFunctionType

EPS = 1e-5
ALPHA = 2.0 ** 0.25
NOPE, ROPE, VD = 128, 64, 128
HCH = 64
ARENA_BYTES = 206 * 1024
NSLOT = 4
SLOT_BYTES = 8192


class Buf:
    __slots__ = ("name", "writer", "rd", "rd_dma", "sem", "semval")

    def __init__(self, name):
        self.name = name
        self.writer = None
        self.rd = {}
        self.rd_dma = []
        self.sem = None
        self.semval = 0


class Op:
    __slots__ = ("eng", "emit", "deps", "ticket", "needs_inc", "is_dma", "dma_sem", "dma_val")

    def __init__(self, eng, emit):
        self.eng = eng
        self.emit = emit
        self.deps = []
        self.ticket = 0
        self.needs_inc = False
        self.is_dma = False
        self.dma_sem = None
        self.dma_val = 0


class Prog:
    ENGS = ("pe", "act", "dve", "pool", "sp")

    def __init__(self, nc, stack):
        self.nc = nc
        self.stack = stack
        self.streams = {e: [] for e in self.ENGS}
        self.esem = {e: stack.enter_context(nc.semaphore("es_" + e)) for e in self.ENGS}
        self.last = {e: None for e in self.ENGS}
        self.dmas = []
        self.nsem = 0
        self.nops = 0

    def _deps(self, op, reads, writes):
        deps = {}
        for b in reads:
            if b.writer is not None:
                deps[id(b.writer)] = b.writer
        for b in writes:
            if b.writer is not None:
                deps[id(b.writer)] = b.writer
            for r in b.rd.values():
                deps[id(r)] = r
            for r in b.rd_dma:
                deps[id(r)] = r
        out = []
        for p in deps.values():
            if p is op:
                continue
            if op.eng == "pe" and p.eng == "pe" and not p.is_dma and not op.is_dma:
                continue
            out.append(p)
            if not p.is_dma:
                p.needs_inc = True
        op.deps = out
        for b in writes:
            b.writer = op
            b.rd = {}
            b.rd_dma = []
        for b in reads:
            if op.is_dma:
                b.rd_dma.append(op)
            else:
                b.rd[op.eng] = op

    def op(self, eng, emit, reads=(), writes=()):
        o = Op(eng, emit)
        self._deps(o, reads, writes)
        self.streams[eng].append(o)
        self.last[eng] = o
        self.nops += 1
        return o

    def dma(self, queue, emits, home, reads=(), writes=()):
        o = Op(queue, emits)
        o.is_dma = True
        if home.sem is None:
            home.sem = self.stack.enter_context(self.nc.semaphore("ds%d" % self.nsem))
            self.nsem += 1
        self._deps(o, reads, writes)
        home.semval += 16 * len(emits)
        o.dma_sem = home.sem
        o.dma_val = home.semval
        self.streams[queue].append(o)
        self.dmas.append(o)
        self.nops += 1
        return o

    def barrier(self, engs=("pe", "act", "dve", "sp")):
        lasts = [self.last[e] for e in engs if self.last[e] is not None]
        dmas = [d for d in self.dmas if d.eng != "pool"]
        self.dmas = []
        new = {}
        for e in engs:
            o = Op(e, None)
            o.deps = [p for p in lasts if not p.is_dma] + dmas
            for p in o.deps:
                if not p.is_dma:
                    p.needs_inc = True
            self.streams[e].append(o)
            new[e] = o

    def wait_all(self, eng, ops):
        o = Op(eng, None)
        o.deps = list(ops)
        for p in ops:
            if not p.is_dma:
                p.needs_inc = True
        self.streams[eng].append(o)

    def emit_all(self):
        nc = self.nc
        for e in self.ENGS:
            c = 0
            for o in self.streams[e]:
                if o.needs_inc and not o.is_dma and o.emit is not None:
                    c += 1
                    o.ticket = c
        esem = self.esem
        streams = self.streams

        def run(ename, eng):
            waited = {}
            for o in streams[ename]:
                need = {}
                for p in o.deps:
                    if p.is_dma:
                        s, v = p.dma_sem, p.dma_val
                    else:
                        s, v = esem[p.eng], p.ticket
                    k = id(s)
                    if k not in need or need[k][1] < v:
                        need[k] = (s, v)
                for k, (s, v) in need.items():
                    if waited.get(k, 0) >= v:
                        continue
                    eng.wait_ge(s, v)
                    waited[k] = v
                if o.emit is None:
                    continue
                if o.is_dma:
                    for em in o.emit:
                        em(eng).then_inc(o.dma_sem, 16)
                else:
                    ins = o.emit(eng)
                    if o.needs_inc:
                        ins.then_inc(esem[ename], 1)

        with nc.Block() as block:
            @block.tensor
            def _(eng):
                run("pe", eng)

            @block.scalar
            def _(eng):
                run("act", eng)

            @block.vector
            def _(eng):
                run("dve", eng)

            @block.gpsimd
            def _(eng):
                run("pool", eng)

            @block.sync
            def _(eng):
                run("sp", eng)


class Cfg:
    def __init__(s, T, D, QL, KVL, H, HH, DFF):
        s.T, s.D, s.QL, s.KVL, s.H, s.HH, s.DFF = T, D, QL, KVL, H, HH, DFF
        s.DC, s.QC, s.KVC, s.DFC = D // 128, QL // 128, KVL // 128, DFF // 128
        s.NT, s.NH = T // 128, T // 512
        s.o_cq = 0
        s.o_ckv = QL
        s.o_kr = QL + KVL
        s.o_hq = s.o_kr + ROPE
        s.o_hf = s.o_hq + HH * 128
        s.o_hi = s.o_hf + HH * 128
        s.o_hg = s.o_hi + HH * 128
        s.o_ga = s.o_hg + HH * 128
        s.o_gb = s.o_ga + D
        s.NIN = s.o_gb + D
        assert T % 512 == 0 and D % 512 == 0 and QL % 256 == 0 and KVL % 128 == 0 and DFF % 256 == 0


def kch(KC, KG):
    return [(k0, min(KG, KC - k0)) for k0 in range(0, KC, KG)]


class Builder:
    def __init__(s, cfg, debug=()):
        s.c = cfg
        s.debug = set(debug)
        s.nc = bass.Bass("TRN2", target_bir_lowering=False)
        s.st = ExitStack()
        s.P = Prog(s.nc, s.st)
        s.ev = 0
        s.slot_ctr = 0
        s.bufcache = {}

    def dram_in(s, name, shape, dt=F32):
        return s.nc.dram_tensor(name, list(shape), dt, kind="ExternalInput").ap()

    def dram_tmp(s, name, shape, dt):
        kind = "ExternalOutput" if name in s.debug else "Internal"
        return s.nc.dram_tensor(name, list(shape), dt, kind=kind).ap()

    def view(s, off, shape, dt, parts=128):
        n = 1
        for d in shape[1:]:
            n *= d
        assert off % 4 == 0
        if dt == BF16:
            a = s.arena[:, off // 2: off // 2 + n]
            nb = 2 * n
        else:
            a = s.arena[:, off // 2: off // 2 + 2 * n].bitcast(dt)
            nb = 4 * n
        assert off + nb <= ARENA_BYTES, (off, nb)
        if len(shape) == 3:
            a = a.rearrange("p (a b) -> p a b", a=shape[1])
        elif len(shape) == 4:
            a = a.rearrange("p (a b c) -> p a b c", a=shape[1], b=shape[2])
        if shape[0] != 128:
            a = a[0:shape[0]]
        return a

    def mm(s, out, lhsT, rhs, start, stop, reads, writes, skip=False):
        if skip:
            return s.P.op("pe", lambda e: e.matmul(out, lhsT=lhsT, rhs=rhs, start=start, stop=stop,
                                                   skip_group_check=True), reads, writes)
        return s.P.op("pe", lambda e: e.matmul(out, lhsT=lhsT, rhs=rhs, start=start, stop=stop), reads, writes)

    def tr(s, out, in_, ident, reads, writes):
        return s.P.op("pe", lambda e: e.transpose(out, in_, ident), reads, writes)

    def act(s, out, in_, func, reads, writes, bias=None, scale=None):
        kw = {}
        if bias is not None:
            kw["bias"] = bias
        if scale is not None:
            kw["scale"] = scale
        return s.P.op("act", lambda e: e.activation(out=out, in_=in_, func=func, **kw), reads, writes)

    def tt(s, out, in0, in1, op, reads, writes, eng="dve"):
        return s.P.op(eng, lambda e: e.tensor_tensor(out=out, in0=in0, in1=in1, op=op), reads, writes)

    def ts(s, out, in0, s1, s2, op0, op1, reads, writes):
        if s2 is None:
            return s.P.op("dve", lambda e: e.tensor_scalar(out=out, in0=in0, scalar1=s1, scalar2=None, op0=op0),
                          reads, writes)
        return s.P.op("dve", lambda e: e.tensor_scalar(out=out, in0=in0, scalar1=s1, scalar2=s2, op0=op0, op1=op1),
                      reads, writes)

    def stt(s, out, in0, scalar, in1, op0, op1, reads, writes):
        return s.P.op("dve", lambda e: e.scalar_tensor_tensor(out=out, in0=in0, scalar=scalar, in1=in1,
                                                              op0=op0, op1=op1), reads, writes)

    def copy(s, out, in_, reads, writes):
        s.ev += 1
        if s.ev % 2 == 0:
            return s.P.op("act", lambda e: e.activation(out=out, in_=in_, func=AF.Copy), reads, writes)
        return s.P.op("dve", lambda e: e.tensor_copy(out=out, in_=in_), reads, writes)

    def dma1(s, queue, out, in_, home, reads, writes):
        return s.P.dma(queue, [lambda e: e.dma_start(out=out, in_=in_)], home, reads, writes)

    def dmag(s, queue, pairs, home, reads, writes):
        ems = [(lambda o_, i_: (lambda e: e.dma_start(out=o_, in_=i_)))(o_, i_) for (o_, i_) in pairs]
        return s.P.dma(queue, ems, home, reads, writes)

    def gran_load(s, pieces, KG, GC):
        slot = s.slot_ctr % NSLOT
        s.slot_ctr += 1
        assert KG * GC * 2 <= SLOT_BYTES
        v = s.view(slot * SLOT_BYTES, [128, KG, GC], BF16)
        b = s.slot_bufs[slot]
        ems = []
        for (dc, W, kc0, nkc, sc, ncol) in pieces:
            wv = W.rearrange("(kc p) n -> p kc n", p=128)
            for a in range(0, nkc, 8):
                n_ = min(8, nkc - a)
                ems.append((lambda o_, i_: (lambda e: e.dma_start(out=o_, in_=i_)))(
                    v[:, a:a + n_, dc:dc + ncol], wv[:, kc0 + a:kc0 + a + n_, sc:sc + ncol]))
        s.P.dma("pool", ems, b, writes=[b])
        return v, b

    class GStream:
        def __init__(g, B, specs):
            g.B, g.specs, g.loaded, g.q = B, specs, 0, []
            for _ in range(min(NSLOT - 1, len(specs))):
                g._issue()

        def _issue(g):
            p, KG, GC = g.specs[g.loaded]
            g.q.append(g.B.gran_load(p, KG, GC))
            g.loaded += 1

        def get(g):
            r = g.q.pop(0)
            if g.loaded < len(g.specs):
                g._issue()
            return r

    def build(s):
        c = s.c
        nc = s.nc
        P = s.P
        T, D, DC, NT, NH = c.T, c.D, c.DC, c.NT, c.NH
        H, HH, QC, KVC, DFC = c.H, c.HH, c.QC, c.KVC, c.DFC
        NCH = T // HCH

        x_ctx = s.dram_in("x_ctx", [T, D])
        x_main = s.dram_in("x_main", [T, D])
        pos_ctx = s.dram_in("pos_ctx", [1, T], I32)
        pos_main = s.dram_in("pos_main", [1, T], I32)
        cst = s.dram_in("cst", [128, 8])
        ident_d = s.dram_in("ident", [128, 128])
        matt_d = s.dram_in("mask_att", [128, 128])
        mhg_d = s.dram_in("mask_hg", [128, 128])
        rst_d = s.dram_in("resetm", [128, T])
        w_in = s.dram_in("w_in", [D, c.NIN])
        w_uq = s.dram_in("w_uq", [c.QL, H * 192])
        w_ukv = s.dram_in("w_ukv", [c.KVL, H * 256])
        w_a = s.dram_in("w_a", [H * 128, D])
        w_b = s.dram_in("w_b", [HH * 128, D])
        w_out = s.dram_in("w_out", [D, D])
        w_gate = s.dram_in("w_gate", [D, c.DFF])
        w_up = s.dram_in("w_up", [D, c.DFF])
        w_down = s.dram_in("w_down", [c.DFF, D])
        lnin_g = s.dram_in("lnin_g", [1, D]); lnin_b = s.dram_in("lnin_b", [1, D])
        ln1_g = s.dram_in("ln1_g", [1, D]); ln1_b = s.dram_in("ln1_b", [1, D])
        ln2_g = s.dram_in("ln2_g", [1, D]); ln2_b = s.dram_in("ln2_b", [1, D])
        qng_d = s.dram_in("qng", [128, QC])
        kvng_d = s.dram_in("kvng", [128, KVC])
        lb_d = s.dram_in("hglb", [128, 2 * HH])
        hgn_d = s.dram_in("hgn", [128, 1])
        gpc_d = s.dram_in("lnin_gpc", [128, DC])
        bpc_d = s.dram_in("lnin_bpc", [128, DC])
        out = nc.dram_tensor("out", [T, D], F32, kind="ExternalOutput").ap()

        hres = s.dram_tmp("hres", [T, D], F32)
        ysc = s.dram_tmp("ysc", [T, D], F32)
        h1res = s.dram_tmp("h1res", [T, D], F32)
        oa_sc = s.dram_tmp("oa_sc", [H, 128, T], BF16)
        ob_sc = s.dram_tmp("ob_sc", [HH, 128, T], BF16)
        mg_sc = s.dram_tmp("mg_sc", [DC, 128, T], BF16)
        act_sc = s.dram_tmp("act_sc", [DFC, 128, T], BF16)
        b_hres = [Buf("hres%d" % i) for i in range(NT)]
        b_ysc = [Buf("ysc%d" % i) for i in range(NT)]
        b_h1res = [Buf("h1res%d" % i) for i in range(NT)]
        b_oasc = [Buf("oasc%d" % i) for i in range(H)]
        b_obsc = [Buf("obsc%d" % i) for i in range(HH)]
        b_mgsc = [Buf("mgsc%d" % i) for i in range(DC)]
        b_actsc = [Buf("actsc%d" % i) for i in range(DFC)]

        s.arena = s.st.enter_context(nc.sbuf_tensor("arena", [128, ARENA_BYTES // 2], BF16))
        s.slot_bufs = [Buf("slot%d" % i) for i in range(NSLOT)]
        ps = [s.st.enter_context(nc.psum_tensor("ps%d" % i, [128, 512], F32))[:, :] for i in range(8)]
        b_ps = [Buf("ps%d" % i) for i in range(8)]

        O_RING = 0
        O_R1 = NSLOT * SLOT_BYTES
        R1_BYTES = DC * T * 2
        O_Z = O_R1 + R1_BYTES
        hT = s.view(O_R1, [128, DC, T], BF16)
        b_r1 = [Buf("r1_%d" % i) for i in range(NT)]

        o = O_Z
        ident = s.view(o, [128, 128], F32); o += 512
        matt = s.view(o, [128, 128], F32); o += 512
        mhg = s.view(o, [128, 128], F32); o += 512
        ones_bf = s.view(o, [128, 128], BF16); o += 256
        identb = s.view(o, [128, 128], BF16); o += 256
        cst_sb = s.view(o, [128, 8], F32); o += 32
        fbias = s.view(o, [128, 1], F32); o += 4
        negpi = s.view(o, [128, 1], F32); o += 4
        epsc = s.view(o, [128, 1], F32); o += 4
        qng = s.view(o, [128, QC], F32); o += 4 * QC
        kvng = s.view(o, [128, KVC], F32); o += 4 * KVC
        lbraw = s.view(o, [128, 2 * HH], F32); o += 8 * HH
        lb = s.view(o, [128, HH], F32); o += 4 * HH
        oml = s.view(o, [128, HH], F32); o += 4 * HH
        hgn = s.view(o, [128, 1], F32); o += 4
        gpc = s.view(o, [128, DC], F32); o += 4 * DC
        bpc = s.view(o, [128, DC], F32); o += 4 * DC
        o = (o + 63) // 64 * 64
        resetm = s.view(o, [128, T], F32); o += 4 * T
        O_P = o
        ckvnT = s.view(o, [128, KVC, 2 * T], BF16); o += KVC * 2 * T * 2
        krope2 = s.view(o, [128, 2 * T], BF16); o += 4 * T
        Sall = s.view(o, [128, HH, 128], F32); o += HH * 512
        O_S = o
        b_const = Buf("const")
        b_ckvn = [Buf("ckvn0"), Buf("ckvn1")]
        b_krope = [Buf("krope0"), Buf("krope1")]
        b_S = [Buf("S%d" % i) for i in range(HH)]
        flag = cst_sb[:, 0:1]
        invf = cst_sb[:, 1:2]
        sgn = cst_sb[:, 3:4]

        s.dmag("sp", [(ident, ident_d), (matt, matt_d), (mhg, mhg_d), (cst_sb, cst), (qng, qng_d),
                      (kvng, kvng_d), (lbraw, lb_d), (hgn, hgn_d), (resetm, rst_d), (gpc, gpc_d), (bpc, bpc_d)],
               b_const, [], [b_const])
        P.op("dve", lambda e: e.memset(ones_bf, 1.0), [], [b_const])
        P.op("dve", lambda e: e.memset(negpi, -math.pi), [], [b_const])
        P.op("dve", lambda e: e.memset(epsc, EPS), [], [b_const])

        def rsqrt_eps(out, in_, reads, writes):
            s.act(out, in_, AF.Sqrt, list(reads) + [b_const], writes, bias=epsc)
            P.op("dve", lambda e: e.reciprocal(out=out, in_=out), writes, writes)
        s.ts(fbias, flag, -1.0, 30000.0, ALU.add, ALU.mult, [b_const], [b_const])
        s.tt(lb, lbraw[:, 0:HH], lbraw[:, HH:2 * HH], ALU.subtract, [b_const], [b_const])
        s.act(lb, lb, AF.Sigmoid, [b_const], [b_const])
        s.ts(oml, lb, -1.0, 1.0, ALU.mult, ALU.add, [b_const], [b_const])

        def ln_stage(src, b_src, g_row, b_row, dst, b_dst, want_T, O_X=None, NB=2, late=None):
            o = O_S if O_X is None else O_X
            xt = [s.view(o + i * 4 * D, [128, D], F32) for i in range(NB)]; o += NB * 4 * D
            if late is None:
                gbc = s.view(o, [128, D], F32); o += 4 * D
                bbc = s.view(o, [128, D], F32); o += 4 * D
            nst = max(1, D // 512)
            stats = [s.view(o + i * nst * 24, [128, nst, 6], F32) for i in range(NB)]; o += NB * nst * 24
            small = [s.view(o + i * 16, [128, 4], F32) for i in range(NB)]; o += NB * 16
            assert o <= ARENA_BYTES, o
            b_xt = [Buf("xt%d" % i) for i in range(NB)]
            b_sm = [Buf("sm%d" % i) for i in range(NB)]
            b_gb = Buf("gb")
            if late is None:
                s.dmag("sp", [(gbc, g_row.broadcast_to([128, D])), (bbc, b_row.broadcast_to([128, D]))], b_gb, [],
                       [b_gb])
            stores = []

            def front(tt_):
                p = tt_ % NB
                X = xt[p]
                rows = slice(tt_ * 128, (tt_ + 1) * 128)
                rd = [b_src[tt_]] if b_src is not None else []
                s.dma1("sp", X, src[rows, :], b_xt[p], rd, [b_xt[p]])
                for k in range(nst):
                    w = min(512, D)
                    P.op("dve", (lambda o_, i_: lambda e: e.bn_stats(out=o_, in_=i_))(
                        stats[p][:, k, :], X[:, k * w:(k + 1) * w]), [b_xt[p]], [b_sm[p]])
                P.op("dve", (lambda o_, i_: lambda e: e.bn_aggr(out=o_, in_=i_))(
                    small[p][:, 0:2], stats[p]), [b_sm[p]], [b_sm[p]])
                rsqrt_eps(small[p][:, 2:3], small[p][:, 1:2], [b_sm[p]], [b_sm[p]])
                s.stt(small[p][:, 3:4], small[p][:, 0:1], -1.0, small[p][:, 2:3], ALU.mult, ALU.mult,
                      [b_sm[p]], [b_sm[p]])
                s.act(X, X, AF.Identity, [b_xt[p], b_sm[p]], [b_xt[p]], bias=small[p][:, 3:4],
                      scale=small[p][:, 2:3])

            def mid(tt_):
                p = tt_ % NB
                X = xt[p]
                rows = slice(tt_ * 128, (tt_ + 1) * 128)
                if late is None:
                    s.tt(X, X, gbc, ALU.mult, [b_xt[p], b_gb], [b_xt[p]])
                    s.tt(X, X, bbc, ALU.add, [b_xt[p], b_gb], [b_xt[p]])
                if dst is not None:
                    stores.append(s.dma1("act", dst[rows, :], X, b_xt[p], [b_xt[p]], [b_dst[tt_]]))

            def back(tt_):
                p = tt_ % NB
                X = xt[p]
                if want_T:
                    for g4 in range(DC // 4):
                        bk = (tt_ * (DC // 4) + g4) % 8
                        for j in range(4):
                            cc = g4 * 4 + j
                            s.tr(ps[bk][:, j * 128:(j + 1) * 128], X[:, cc * 128:(cc + 1) * 128], ident,
                                 [b_xt[p], b_const], [b_ps[bk]])
                        if late is None:
                            s.copy(hT[:, g4 * 4:(g4 + 1) * 4, tt_ * 128:(tt_ + 1) * 128],
                                   ps[bk].rearrange("p (a b) -> p a b", a=4), [b_ps[bk]], [b_r1[tt_]])
                        else:
                            gpc_, bpc_ = late
                            for j in range(4):
                                cc = g4 * 4 + j
                                o_ = hT[:, cc, tt_ * 128:(tt_ + 1) * 128]
                                i_ = ps[bk][:, j * 128:(j + 1) * 128]
                                if j % 2 == 0:
                                    s.act(o_, i_, AF.Identity, [b_ps[bk], b_const], [b_r1[tt_]],
                                          bias=bpc_[:, cc:cc + 1], scale=gpc_[:, cc:cc + 1])
                                else:
                                    s.ts(o_, i_, gpc_[:, cc:cc + 1], bpc_[:, cc:cc + 1], ALU.mult, ALU.add,
                                         [b_ps[bk], b_const], [b_r1[tt_]])

            if NB >= 3:
                for i in range(NT + 2):
                    if i < NT:
                        front(i)
                    if 0 <= i - 1 < NT:
                        mid(i - 1)
                    if 0 <= i - 2 < NT:
                        back(i - 2)
            else:
                for i in range(NT + 1):
                    if i < NT:
                        front(i)
                    if 0 <= i - 1 < NT:
                        mid(i - 1)
                        back(i - 1)
            return stores

        def fm_acc(gv, gb, a, b_, nk, kg0, KC, banks, actT, actb, t0=0, M=None):
            M = (b_ - a) if M is None else M
            for k in range(nk):
                kk = kg0 + k
                for th, bk in enumerate(banks):
                    s.mm(ps[bk][0:M, :], gv[:, k, a:b_], actT[:, kk, t0 + th * 512: t0 + (th + 1) * 512],
                         kk == 0, kk == KC - 1, [gb] + actb(th), [b_ps[bk]])

        def r1b(th):
            return b_r1[th * 4:(th + 1) * 4]

        def rms_fm(raw, b_raw, ncn, gcol, outT_fn, b_out, nth, sq, b_sq, rstd, b_rstd, n):
            for th in range(nth):
                tsl = slice(th * 512, (th + 1) * 512)
                for ch in range(ncn):
                    s.act(sq[ch % 2], raw[:, ch, tsl], AF.Square, [b_raw], [b_sq[ch % 2]], scale=float(n) ** -0.5)
                    s.mm(ps[7][:, :], ones_bf, sq[ch % 2], ch == 0, ch == ncn - 1, [b_sq[ch % 2], b_const], [b_ps[7]])
                rsqrt_eps(rstd, ps[7][:, :], [b_ps[7]], [b_rstd])
                for ch in range(ncn):
                    s.stt(outT_fn(ch, th), raw[:, ch, tsl], gcol[:, ch:ch + 1], rstd, ALU.mult, ALU.mult,
                          [b_raw, b_rstd, b_const], [b_out])

        def rope_tables(pos_d, cos4, ssin4, b_tab, scratch_i, scratch_f, b_scr):
            s.dma1("sp", scratch_i, pos_d.broadcast_to([128, T]), b_scr, [], [b_scr])
            P.op("dve", lambda e: e.tensor_copy(out=scratch_f, in_=scratch_i), [b_scr], [b_scr])
            s.ts(scratch_f, scratch_f, invf, None, ALU.mult, None, [b_scr, b_const], [b_scr])

            def sin_of(dst, shift):
                s.ts(dst, scratch_f, shift, 1.0 / (2 * math.pi), ALU.add, ALU.mult, [b_scr], [b_tab])
                P.op("dve", lambda e: e.tensor_copy(out=scratch_i, in_=dst), [b_tab, b_scr], [b_scr2])
                P.op("dve", lambda e: e.tensor_copy(out=dst, in_=scratch_i), [b_scr2], [b_tab])
                s.ts(dst, dst, -2 * math.pi, shift, ALU.mult, ALU.add, [b_tab], [b_tab])
                s.tt(dst, dst, scratch_f, ALU.add, [b_tab, b_scr], [b_tab])
                s.ts(dst, dst, -3.14159, 3.14159, ALU.max, ALU.min, [b_tab], [b_tab])
                s.act(dst, dst, AF.Sin, [b_tab], [b_tab])

            b_scr2 = Buf("scr2")
            sin_of(ssin4, 0.0)
            s.ts(ssin4, ssin4, sgn, None, ALU.mult, None, [b_tab, b_const], [b_tab])
            sin_of(cos4, 0.5 * math.pi)

        def proj_small(hf, pos_d, want_q, O_X, cqnT, b_cqn, tq, b_tq):
            o = O_X
            cos4 = s.view(o, [128, T], F32); o += 4 * T
            ssin4 = s.view(o, [128, T], F32); o += 4 * T
            sci = s.view(o, [128, T], I32); o += 4 * T
            scf = s.view(o, [128, T], F32); o += 4 * T
            nraw = max(KVC, QC if want_q else 0)
            raw = s.view(o, [128, nraw, T], F32); o += nraw * T * 4
            sq = [s.view(o + i * 1024, [128, 512], BF16) for i in range(2)]; o += 2048
            rstd = s.view(o, [128, 512], F32); o += 2048
            ta = s.view(o, [128, 512], F32); o += 2048
            b_tab, b_scr, b_raw, b_rstd, b_ta = Buf("tab"), Buf("scr"), Buf("raw"), Buf("rstd"), Buf("ta")
            b_sq = [Buf("sq0"), Buf("sq1")]
            rope_tables(pos_d, cos4, ssin4, b_tab, sci, scf, b_scr)
            if want_q:
                P.op("dve", lambda e: e.tensor_copy(out=tq[0:64, :], in_=cos4[0:64, :]), [b_tab], [b_tq])
                P.op("dve", lambda e: e.tensor_copy(out=tq[64:128, :], in_=ssin4[64:128, :]), [b_tab], [b_tq])
            specs = []
            if want_q:
                for pnl in range(c.QL // 256):
                    for (k0, nk) in kch(DC, 16):
                        specs.append(([(0, w_in, k0, nk, c.o_cq + pnl * 256, 256)], 16, 256))
            for pnl in range(c.KVL // 128):
                specs.append(([(0, w_in, 0, DC, c.o_ckv + pnl * 128, 128)], DC, 128))
            kr = c.o_kr
            for (k0, nk) in kch(DC, 16):
                specs.append(([(0, w_in, k0, nk, kr, 64), (64, w_in, k0, nk, kr, 64),
                               (128, w_in, k0, nk, kr + 32, 32), (160, w_in, k0, nk, kr, 32),
                               (192, w_in, k0, nk, kr + 32, 32), (224, w_in, k0, nk, kr, 32)], 16, 256))
            gs = Builder.GStream(s, specs)
            if want_q:
                for pnl in range(c.QL // 256):
                    for (k0, nk) in kch(DC, 16):
                        gv, gb = gs.get()
                        for ct in range(2):
                            fm_acc(gv, gb, ct * 128, (ct + 1) * 128, nk, k0, DC,
                                   [ct * NH + th for th in range(NH)], hT, r1b)
                    for ct in range(2):
                        for th in range(NH):
                            s.copy(raw[:, pnl * 2 + ct, th * 512:(th + 1) * 512], ps[ct * NH + th][:, :],
                                   [b_ps[ct * NH + th]], [b_raw])
                rms_fm(raw, b_raw, QC, qng, lambda ch, th: cqnT[:, ch, th * 512:(th + 1) * 512], b_cqn, NH,
                       sq, b_sq, rstd, b_rstd, c.QL)
            for pnl in range(c.KVL // 128):
                gv, gb = gs.get()
                fm_acc(gv, gb, 0, 128, DC, 0, DC, [4 + th for th in range(NH)], hT, r1b)
                for th in range(NH):
                    s.copy(raw[:, pnl, th * 512:(th + 1) * 512], ps[4 + th][:, :], [b_ps[4 + th]], [b_raw])
            rms_fm(raw, b_raw, KVC, kvng,
                   lambda ch, th: ckvnT[:, ch, hf * T + th * 512: hf * T + (th + 1) * 512], b_ckvn[hf], NH,
                   sq, b_sq, rstd, b_rstd, c.KVL)
            for (k0, nk) in kch(DC, 16):
                gv, gb = gs.get()
                fm_acc(gv, gb, 0, 128, nk, k0, DC, [th for th in range(NH)], hT, r1b)
                fm_acc(gv, gb, 128, 256, nk, k0, DC, [2 + th for th in range(NH)], hT, r1b)
            for th in range(NH):
                tsl = slice(th * 512, (th + 1) * 512)
                s.tt(ta, ps[th][:, :], cos4[:, tsl], ALU.mult, [b_ps[th], b_tab], [b_ta])
                s.tt(rstd, ps[2 + th][:, :], ssin4[:, tsl], ALU.mult, [b_ps[2 + th], b_tab], [b_rstd])
                s.tt(krope2[:, hf * T + th * 512: hf * T + (th + 1) * 512], ta, rstd, ALU.add,
                     [b_ta, b_rstd], [b_krope[hf]])

        def hgrn(hf, main, O_X):
            o = O_X
            Fb = [s.view(o + i * 4 * T, [128, T], F32) for i in range(4)]; o += 16 * T
            G = [s.view(o + i * 4 * T, [128, T], F32) for i in range(2)]; o += 8 * T
            qtT2 = [s.view(o + i * 2 * T, [128, T], BF16) for i in range(2)]; o += 4 * T
            vsb2 = [s.view(o + i * 2 * T, [128, NT, 128], BF16) for i in range(2)]; o += 4 * T
            ktT = s.view(o, [128, T], BF16); o += 2 * T
            khT = s.view(o, [128, T], BF16); o += 2 * T
            khat = s.view(o, [128, NT, 128], BF16); o += 2 * T
            ATs = s.view(o, [128, NT, 128], BF16); o += 2 * T
            obs = s.view(o, [128, T], BF16); o += 2 * T
            eB2 = [s.view(o + i * 4 * NCH, [128, NCH], F32) for i in range(2)]; o += 8 * NCH
            Sf = s.view(o, [128, NCH + 1, 128], F32); o += (NCH + 1) * 512
            Sbl = s.view(o, [128, NCH, 128], BF16); o += NCH * 256
            sq = s.view(o, [128, 512], BF16); o += 1024
            rstd = s.view(o, [128, 512], F32); o += 2048
            tmp = s.view(o, [128, 512], F32); o += 2048
            assert o <= ARENA_BYTES, o
            bF = [Buf("F%d" % i) for i in range(4)]
            b_G = [Buf("G0"), Buf("G1")]
            b_qt2 = [Buf("qt0"), Buf("qt1")]
            b_v2 = [Buf("v0"), Buf("v1")]
            b_eB2 = [Buf("eB0"), Buf("eB1")]
            b_kt, b_kh, b_khat, b_AT, b_obs, b_Sb, b_Sf, b_sq, b_rstd, b_tmp = [Buf(n) for n in (
                "kt", "kh", "khat", "AT", "obs", "Sb", "Sf", "sq", "rstd", "tmp")]
            F0, F1, F2, F3 = Fb
            specs = []
            for h in range(HH):
                cols = [c.o_hf + h * 128] + ([c.o_hq + h * 128] if main else [])
                for (k0, nk) in kch(DC, 16):
                    specs.append(([(i * 128, w_in, k0, nk, cc, 128) for i, cc in enumerate(cols)], 16, 256))
                specs.append(([(0, w_in, 0, DC, c.o_hi + h * 128, 128)], DC, 128))
                if main:
                    specs.append(([(0, w_in, 0, DC, c.o_hg + h * 128, 128)], DC, 128))
            gs = Builder.GStream(s, specs)
            F3c = F3.rearrange("p (c l) -> p c l", l=HCH)
            F2c = F2.rearrange("p (c l) -> p c l", l=HCH)
            nU = NCH - 1 if main else NCH

            def ubank(ck):
                return 4 + (ck % 2) + 2 * ((ck // 2) // 4), ((ck // 2) % 4) * 128

            def ph_P1(h):
                for (k0, nk) in kch(DC, 16):
                    gv, gb = gs.get()
                    fm_acc(gv, gb, 0, 128, nk, k0, DC, [th for th in range(NH)], hT, r1b)
                    if main:
                        fm_acc(gv, gb, 128, 256, nk, k0, DC, [2 + th for th in range(NH)], hT, r1b)
                for th in range(NH):
                    tsl = slice(th * 512, (th + 1) * 512)
                    s.act(F1[:, tsl], ps[th][:, :], AF.Sigmoid, [b_ps[th]], [bF[1]])
                    if main:
                        s.act(F0[:, tsl], ps[2 + th][:, :], AF.Silu, [b_ps[2 + th]], [bF[0]])

            def ph_P23(h):
                vsb = vsb2[h % 2]
                vb0 = 0 if main else 2
                gv, gb = gs.get()
                for t_ in range(NT):
                    bk = vb0 + t_ // 4
                    for k in range(DC):
                        s.mm(ps[bk][:, (t_ % 4) * 128:(t_ % 4 + 1) * 128], hT[:, k, t_ * 128:(t_ + 1) * 128],
                             gv[:, k, 0:128], k == 0, k == DC - 1, [gb, b_r1[t_]], [b_ps[bk]])
                for g4 in range(NT // 4):
                    s.copy(vsb[:, g4 * 4:(g4 + 1) * 4, :], ps[vb0 + g4].rearrange("p (a b) -> p a b", a=4),
                           [b_ps[vb0 + g4]], [b_v2[h % 2]])
                if main:
                    gv, gb = gs.get()
                    fm_acc(gv, gb, 0, 128, DC, 0, DC, [2 + th for th in range(NH)], hT, r1b)
                    for th in range(NH):
                        s.act(G[h % 2][:, th * 512:(th + 1) * 512], ps[2 + th][:, :], AF.Silu, [b_ps[2 + th]],
                              [b_G[h % 2]])

            def ph_E(h):
                qtT, b_qt = qtT2[h % 2], b_qt2[h % 2]
                eB, b_eB = eB2[h % 2], b_eB2[h % 2]
                s.ts(F1, F1, oml[:, h:h + 1], lb[:, h:h + 1], ALU.mult, ALU.add, [bF[1], b_const], [bF[1]])
                s.act(F2, F1, AF.Ln, [bF[1]], [bF[2]])
                s.ts(F1, F1, -1.0, 1.0, ALU.mult, ALU.add, [bF[1]], [bF[1]])
                P.op("dve", lambda e: e.tensor_tensor_scan(out=F3, data0=resetm, data1=F2, initial=0.0,
                                                           op0=ALU.mult, op1=ALU.add),
                     [bF[2], b_const], [bF[3]])
                if main:
                    s.act(F2, F3, AF.Exp, [bF[3]], [bF[2]])
                    s.tt(qtT, F0, F2, ALU.mult, [bF[0], bF[2]], [b_qt])
                    s.act(F2, F3, AF.Exp, [bF[3]], [bF[2]], scale=-1.0)
                    s.tt(ktT, F1, F2, ALU.mult, [bF[1], bF[2]], [b_kt])
                s.act(eB, F3c[:, :, HCH - 1], AF.Exp, [bF[3]], [b_eB])
                s.tt(F2c, F3c, F3c[:, :, HCH - 1:HCH].to_broadcast([128, NCH, HCH]), ALU.subtract,
                     [bF[3]], [bF[2]])
                s.act(F2, F2, AF.Exp, [bF[2]], [bF[2]], scale=-1.0)
                s.tt(khT, F1, F2, ALU.mult, [bF[1], bF[2]], [b_kh])

            def ph_R1(h):
                qtT, b_qt = qtT2[h % 2], b_qt2[h % 2]
                vsb, b_v = vsb2[h % 2], b_v2[h % 2]
                eB, b_eB = eB2[h % 2], b_eB2[h % 2]
                for g4 in range(NT // 4):
                    bk = 4 + g4
                    pv = ps[bk][:, 0:256].bitcast(BF16)
                    for j in range(4):
                        t_ = g4 * 4 + j
                        s.tr(pv[:, j * 128:(j + 1) * 128], khT[:, t_ * 128:(t_ + 1) * 128], identb,
                             [b_kh, b_const], [b_ps[bk]])
                    s.copy(khat[:, g4 * 4:(g4 + 1) * 4, :], pv.rearrange("p (a b) -> p a b", a=4),
                           [b_ps[bk]], [b_khat])
                S = Sall[:, h, :]
                if not main:
                    P.op("dve", (lambda S_: lambda e: e.memset(S_, 0.0))(S), [], [b_S[h]])
                if main:
                    for g4 in range(NT // 4):
                        bk = 6 + g4
                        for j in range(4):
                            t_ = g4 * 4 + j
                            tl = slice(t_ * 128, (t_ + 1) * 128)
                            s.mm(ps[bk][:, j * 128:(j + 1) * 128], ktT[:, tl], qtT[:, tl], True, True,
                                 [b_kt, b_qt], [b_ps[bk]])
                        s.tt(ATs[:, g4 * 4:(g4 + 1) * 4, :], ps[bk].rearrange("p (a b) -> p a b", a=4),
                             mhg.unsqueeze(1).to_broadcast([128, 4, 128]), ALU.mult, [b_ps[bk], b_const], [b_AT])
                for ck in range(nU):
                    t_ = ck // 2
                    pr = slice((ck % 2) * 64, (ck % 2) * 64 + 64)
                    ubk, uc = ubank(ck)
                    s.mm(ps[ubk][:, uc:uc + 128], khat[pr, t_, :], vsb[pr, t_, :], True, True,
                         [b_khat, b_v], [b_ps[ubk]])
                P.op("dve", (lambda S_: lambda e: e.tensor_copy(out=Sf[:, 0, :], in_=S_))(S), [b_S[h]], [b_Sf])
                for ck in range(nU):
                    ubk, uc = ubank(ck)
                    last = (not main and ck == nU - 1)
                    dst_ = S if last else Sf[:, ck + 1, :]
                    wr = [b_S[h]] if last else [b_Sf]
                    s.stt(dst_, Sf[:, ck, :], eB[:, ck:ck + 1], ps[ubk][:, uc:uc + 128],
                          ALU.mult, ALU.add, [b_Sf, b_eB, b_ps[ubk]], wr)
                if main:
                    s.act(Sbl, Sf[:, 0:NCH, :], AF.Copy, [b_Sf], [b_Sb])
                else:
                    s.ts(S, S, flag, None, ALU.mult, None, [b_S[h], b_const], [b_S[h]])

            def ph_R2(h):
                qtT, b_qt = qtT2[h % 2], b_qt2[h % 2]
                vsb, b_v = vsb2[h % 2], b_v2[h % 2]
                for t_ in range(NT):
                    ob_bk = 4 + t_ // 4
                    col0 = (t_ % 4) * 128
                    s.mm(ps[ob_bk][:, col0:col0 + 128], vsb[:, t_, :], ATs[:, t_, :], True, False,
                         [b_v, b_AT], [b_ps[ob_bk]])
                    for cc in range(2):
                        ck = 2 * t_ + cc
                        cs = col0 + cc * 64
                        s.mm(ps[ob_bk][:, cs:cs + 64], Sbl[:, ck, :], qtT[:, ck * 64:(ck + 1) * 64], False,
                             cc == 1, [b_Sb, b_qt], [b_ps[ob_bk]])
                for th in range(NH):
                    tsl = slice(th * 512, (th + 1) * 512)
                    obk = 4 + th
                    s.act(sq, ps[obk][:, :], AF.Square, [b_ps[obk]], [b_sq], scale=128.0 ** -0.5)
                    s.mm(ps[6][:, :], ones_bf, sq, True, True, [b_sq, b_const], [b_ps[6]])
                    rsqrt_eps(rstd, ps[6][:, :], [b_ps[6]], [b_rstd])
                    s.stt(tmp, ps[obk][:, :], hgn[:, 0:1], rstd, ALU.mult, ALU.mult, [b_ps[obk], b_rstd, b_const],
                          [b_tmp])
                    s.tt(obs[:, tsl], tmp, G[h % 2][:, tsl], ALU.mult, [b_tmp, b_G[h % 2]], [b_obs])
                s.dma1("sp", ob_sc[h], obs, b_obs, [b_obs], [b_obsc[h]])

            for i in range(HH + 1):
                if i < HH:
                    ph_P1(i)
                if i >= 1:
                    ph_R1(i - 1)
                if i < HH:
                    ph_P23(i)
                if i >= 1 and main:
                    ph_R2(i - 1)
                if i < HH:
                    ph_E(i)

        def attention(O_X, cqnT, b_cqn, tq, b_tq):
            o = O_X
            qn2 = [s.view(o + i * 2 * T, [128, T], BF16) for i in range(2)]; o += 4 * T
            Rq2 = [s.view(o + i * 2 * T, [128, T], BF16) for i in range(2)]; o += 4 * T
            kT2 = [s.view(o + i * 4 * T, [128, 2 * T], BF16) for i in range(2)]; o += 8 * T
            vv2 = [s.view(o + i * 4 * T, [128, 2 * NT, 128], BF16) for i in range(2)]; o += 8 * T
            pT = [s.view(o + i * 1024, [128, 512], BF16) for i in range(2)]; o += 2048
            rec = s.view(o, [128, 512], F32); o += 2048
            oas = s.view(o, [128, T], BF16); o += 2 * T
            assert o <= ARENA_BYTES, o
            b_rec, b_oas = Buf("rec"), Buf("oas")
            b_qn2, b_Rq2, b_kT2, b_vv2 = [[Buf(n + "0"), Buf(n + "1")] for n in ("qn", "Rq", "kT", "vv")]
            b_pT = [Buf("pT0"), Buf("pT1")]
            scale = float(NOPE + ROPE) ** -0.5
            specs = []
            for h in range(H):
                q0 = h * 192
                specs.append(([(0, w_uq, 0, QC, q0, 128), (128, w_uq, 0, QC, q0 + 128, 64),
                               (192, w_uq, 0, QC, q0 + 160, 32), (224, w_uq, 0, QC, q0 + 128, 32)], QC, 256))
                specs.append(([(0, w_ukv, 0, KVC, h * 256, 256)], KVC, 256))
            gs = Builder.GStream(s, specs)
            NK = 2 * NT
            blk = 0
            def att_proj(h):
                qn, Rq, kT, vv = qn2[h % 2], Rq2[h % 2], kT2[h % 2], vv2[h % 2]
                b_qn, b_Rq, b_kT, b_vv = b_qn2[h % 2], b_Rq2[h % 2], b_kT2[h % 2], b_vv2[h % 2]
                gv, gb = gs.get()
                for th in range(NH):
                    for k in range(QC):
                        s.mm(ps[th][:, :], gv[:, k, 0:128], cqnT[:, k, th * 512:(th + 1) * 512], k == 0, k == QC - 1,
                             [gb, b_cqn], [b_ps[th]])
                    for k in range(QC):
                        s.mm(ps[2 + th][:, :], gv[:, k, 128:256], cqnT[:, k, th * 512:(th + 1) * 512], k == 0,
                             k == QC - 1, [gb, b_cqn], [b_ps[2 + th]])
                for th in range(NH):
                    tsl = slice(th * 512, (th + 1) * 512)
                    s.copy(qn[:, tsl], ps[th][:, :], [b_ps[th]], [b_qn])
                    s.tt(Rq[:, tsl], ps[2 + th][:, :], tq[:, tsl], ALU.mult, [b_ps[2 + th], b_tq], [b_Rq])
                gv, gb = gs.get()
                for kg in range(2 * NH):
                    bk = kg % 4
                    for k in range(KVC):
                        s.mm(ps[bk][:, :], gv[:, k, 0:128], ckvnT[:, k, kg * 512:(kg + 1) * 512], k == 0,
                             k == KVC - 1, [gb] + b_ckvn, [b_ps[bk]])
                    s.copy(kT[:, kg * 512:(kg + 1) * 512], ps[bk][:, :], [b_ps[bk]], [b_kT])
                for g4 in range(NK // 4):
                    bk = g4 % 4
                    for j in range(4):
                        kt = g4 * 4 + j
                        for k in range(KVC):
                            s.mm(ps[bk][:, j * 128:(j + 1) * 128], ckvnT[:, k, kt * 128:(kt + 1) * 128],
                                 gv[:, k, 128:256], k == 0, k == KVC - 1, [gb] + b_ckvn, [b_ps[bk]])
                    s.copy(vv[:, g4 * 4:(g4 + 1) * 4, :], ps[bk].rearrange("p (a b) -> p a b", a=4),
                           [b_ps[bk]], [b_vv])

            def att_blocks(h):
                nonlocal blk
                qn, Rq, kT, vv = qn2[h % 2], Rq2[h % 2], kT2[h % 2], vv2[h % 2]
                b_qn, b_Rq, b_kT, b_vv = b_qn2[h % 2], b_Rq2[h % 2], b_kT2[h % 2], b_vv2[h % 2]
                for g in range(NH):
                    last_kt = NT + 4 * g + 3
                    bls = []
                    for kt in range(last_kt + 1):
                        is_ctx = kt < NT
                        off = 0 if is_ctx else max(0, (kt - NT) * 128 - g * 512)
                        bls.append((kt, is_ctx, off, 4 + blk % 2, blk % 2))
                        blk += 1

                    def emit_S(bl, g=g):
                        kt, is_ctx, off, sb, pb = bl
                        n = 512 - off
                        q0 = g * 512 + off
                        s.mm(ps[sb][:, off:512], kT[:, kt * 128:(kt + 1) * 128], qn[:, q0:q0 + n], True, False,
                             [b_kT, b_qn], [b_ps[sb]])
                        s.mm(ps[sb][:, off:512], krope2[:, kt * 128:(kt + 1) * 128], Rq[:, q0:q0 + n], False, True,
                             b_krope + [b_Rq], [b_ps[sb]])

                    def emit_exp(bl, g=g):
                        kt, is_ctx, off, sb, pb = bl
                        if is_ctx:
                            s.act(pT[pb][:, off:512], ps[sb][:, off:512], AF.Exp, [b_ps[sb], b_const], [b_pT[pb]],
                                  bias=fbias, scale=scale)
                        else:
                            s.act(pT[pb][:, off:512], ps[sb][:, off:512], AF.Exp, [b_ps[sb]], [b_pT[pb]], scale=scale)
                            if (kt - NT) * 128 >= g * 512:
                                s.tt(pT[pb][:, off:off + 128], pT[pb][:, off:off + 128], matt, ALU.mult,
                                     [b_pT[pb], b_const], [b_pT[pb]])

                    def emit_PV(bl, first, last):
                        kt, is_ctx, off, sb, pb = bl
                        s.mm(ps[6][:, off:512], vv[:, kt, :], pT[pb][:, off:512], first, last,
                             [b_vv, b_pT[pb]], [b_ps[6]])
                        s.mm(ps[7][:, off:512], ones_bf, pT[pb][:, off:512], first, last,
                             [b_const, b_pT[pb]], [b_ps[7]])

                    emit_S(bls[0])
                    for i, bl in enumerate(bls):
                        if i + 1 < len(bls):
                            emit_S(bls[i + 1])
                        emit_exp(bl)
                        emit_PV(bl, i == 0, i == len(bls) - 1)
                    P.op("dve", lambda e: e.reciprocal(out=rec, in_=ps[7][:, :]), [b_ps[7]], [b_rec])
                    s.tt(oas[:, g * 512:(g + 1) * 512], ps[6][:, :], rec, ALU.mult, [b_ps[6], b_rec], [b_oas])
                s.dma1("sp", oa_sc[h], oas, b_oas, [b_oas], [b_oasc[h]])

            att_proj(0)
            for h in range(H):
                if h + 1 < H:
                    att_proj(h + 1)
                att_blocks(h)

        def merge(O_X):
            o = O_X
            HC, HHC = H, HH
            oaT = s.view(o, [128, HC, T], BF16); o += HC * T * 2
            obT = s.view(o, [128, HHC, T], BF16); o += HHC * T * 2
            m1 = s.view(o, [128, 2, T], F32); o += 8 * T
            sg = [s.view(o + i * 2048, [128, 512], F32) for i in range(2)]; o += 4096
            t2 = [s.view(o + i * 2048, [128, 512], F32) for i in range(2)]; o += 4096
            mst = [s.view(o + i * 2 * T, [128, T], BF16) for i in range(2)]; o += 4 * T
            assert o <= ARENA_BYTES, o
            b_oa, b_ob, b_m1 = Buf("oaT"), Buf("obT"), [Buf("m1a"), Buf("m1b")]
            b_sg = [Buf("sg0"), Buf("sg1")]
            b_t2 = [Buf("t20"), Buf("t21")]
            b_mst = [Buf("mst0"), Buf("mst1")]
            s.dmag("sp", [(oaT[:, h, :], oa_sc[h]) for h in range(HC)], b_oa, b_oasc, [b_oa])
            s.dmag("sp", [(obT[:, h, :], ob_sc[h]) for h in range(HHC)], b_ob, b_obsc, [b_ob])
            specs = []
            npan = D // 256
            for pn in range(npan):
                for (Wm, KCm, wgc, goff) in ((w_a, HC, None, c.o_ga), (w_b, HHC, None, c.o_gb)):
                    for (k0, nk) in kch(KCm, 16):
                        specs.append(([(0, Wm, k0, nk, pn * 256, 256)], 16, 256))
                    for (k0, nk) in kch(DC, 16):
                        specs.append(([(0, w_in, k0, nk, goff + pn * 256, 256)], 16, 256))
            gs = Builder.GStream(s, specs)
            ui = 0
            for pn in range(npan):
                for step, (srcT, b_src, KCm) in enumerate(((oaT, b_oa, HC), (obT, b_ob, HHC))):
                    def bank(ct, which, th):
                        return ct * 4 + which * 2 + th
                    for (k0, nk) in kch(KCm, 16):
                        gv, gb = gs.get()
                        for ct in range(2):
                            fm_acc(gv, gb, ct * 128, (ct + 1) * 128, nk, k0, KCm,
                                   [bank(ct, 0, th) for th in range(NH)], srcT, (lambda bb: (lambda th: [bb]))(b_src))
                    for (k0, nk) in kch(DC, 16):
                        gv, gb = gs.get()
                        for ct in range(2):
                            fm_acc(gv, gb, ct * 128, (ct + 1) * 128, nk, k0, DC,
                                   [bank(ct, 1, th) for th in range(NH)], hT, r1b)
                    for ct in range(2):
                        for th in range(NH):
                            tsl = slice(th * 512, (th + 1) * 512)
                            u = ui % 2
                            ui += 1
                            s.act(sg[u], ps[bank(ct, 1, th)][:, :], AF.Sigmoid, [b_ps[bank(ct, 1, th)]], [b_sg[u]])
                            if step == 0:
                                s.tt(m1[:, ct, tsl], ps[bank(ct, 0, th)][:, :], sg[u], ALU.mult,
                                     [b_ps[bank(ct, 0, th)], b_sg[u]], [b_m1[ct]])
                            else:
                                s.tt(t2[u], ps[bank(ct, 0, th)][:, :], sg[u], ALU.mult,
                                     [b_ps[bank(ct, 0, th)], b_sg[u]], [b_t2[u]])
                                s.tt(mst[ct][:, tsl], t2[u], m1[:, ct, tsl], ALU.add, [b_t2[u], b_m1[ct]],
                                     [b_mst[ct]])
                        if step == 1:
                            s.dma1("sp", mg_sc[pn * 2 + ct], mst[ct], b_mst[ct], [b_mst[ct]], [b_mgsc[pn * 2 + ct]])

        def gemm_tm(W, KC, actT, actb, tiles, row0, resid, b_res, dst, b_dst, O_X, sets, kbufs=None, aff=None,
                    wk0=0, alpha=ALPHA):
            ntl = len(tiles)
            rs = [[s.view(O_X + (p * ntl + j) * 2048, [128, 512], F32) for j in range(ntl)] for p in range(2)]
            key = ("rs", O_X, ntl)
            if key not in s.bufcache:
                s.bufcache[key] = [[Buf("rs%d_%d" % (p, j)) for j in range(ntl)] for p in range(2)]
            b_rs = s.bufcache[key]
            ncg = D // 512
            KG = 8
            ngr = (KC + KG - 1) // KG
            specs = []
            for cg in range(ncg):
                for gi in range(ngr):
                    nk = min(KG, KC - gi * KG)
                    specs.append(([(0, W, wk0 + gi * KG, nk, cg * 512, 512)], KG, 512))
            gs = Builder.GStream(s, specs)
            stores = []
            for cg in range(ncg):
                p = cg % 2
                csl = slice(cg * 512, (cg + 1) * 512)
                for j, t_ in enumerate(tiles):
                    r0 = row0 + j * 128
                    s.dma1("sp", rs[p][j], resid[r0:r0 + 128, csl], b_rs[p][j], [b_res[(r0 // 128)]], [b_rs[p][j]])
                    if aff is not None:
                        ga_, ba_, b_aff = aff
                        s.tt(rs[p][j], rs[p][j], ga_[:, csl], ALU.mult, [b_rs[p][j], b_aff], [b_rs[p][j]])
                        s.tt(rs[p][j], rs[p][j], ba_[:, csl], ALU.add, [b_rs[p][j], b_aff], [b_rs[p][j]])
                for gi in range(ngr):
                    nk = min(KG, KC - gi * KG)
                    gv, gb = gs.get()
                    for k in range(nk):
                        kk = gi * KG + k
                        for j, t_ in enumerate(tiles):
                            bk = (p * ntl + j) if sets == 2 else j
                            ab = [kbufs[kk // 8]] if kbufs is not None else actb(t_)
                            s.mm(ps[bk][:, :], actT[:, kk, t_ * 128:(t_ + 1) * 128], gv[:, k, 0:512], kk == 0,
                                 kk == KC - 1, [gb] + ab, [b_ps[bk]])
                for j, t_ in enumerate(tiles):
                    bk = (p * ntl + j) if sets == 2 else j
                    r0 = row0 + j * 128
                    s.stt(rs[p][j], rs[p][j], alpha, ps[bk][:, :], ALU.mult, ALU.add, [b_rs[p][j], b_ps[bk]],
                          [b_rs[p][j]])
                    stores.append(s.dma1("act", dst[r0:r0 + 128, csl], rs[p][j], b_rs[p][j], [b_rs[p][j]],
                                         [b_dst[r0 // 128]]))
            return stores

        def ffn1(O_X):
            o = O_X
            sl = [s.view(o + i * 2048, [128, 512], F32) for i in range(2)]; o += 4096
            ast = [s.view(o + i * 2 * T, [128, T], BF16) for i in range(2)]; o += 4 * T
            b_sl = [Buf("sl0"), Buf("sl1")]
            b_ast = [Buf("ast0"), Buf("ast1")]
            npan = c.DFF // 256
            specs = []
            for pn in range(npan):
                for Wm in (w_gate, w_up):
                    for (k0, nk) in kch(DC, 16):
                        specs.append(([(0, Wm, k0, nk, pn * 256, 256)], 16, 256))
            gs = Builder.GStream(s, specs)
            ui = 0
            for pn in range(npan):
                def bank(ct, which, th):
                    return ct * 4 + which * 2 + th
                for which in range(2):
                    for (k0, nk) in kch(DC, 16):
                        gv, gb = gs.get()
                        for ct in range(2):
                            fm_acc(gv, gb, ct * 128, (ct + 1) * 128, nk, k0, DC,
                                   [bank(ct, which, th) for th in range(NH)], hT, r1b)
                for ct in range(2):
                    a = (pn * 2 + ct) % 2
                    for th in range(NH):
                        tsl = slice(th * 512, (th + 1) * 512)
                        u = ui % 2
                        ui += 1
                        s.act(sl[u], ps[bank(ct, 0, th)][:, :], AF.Silu, [b_ps[bank(ct, 0, th)]], [b_sl[u]])
                        s.tt(ast[a][:, tsl], ps[bank(ct, 1, th)][:, :], sl[u], ALU.mult,
                             [b_ps[bank(ct, 1, th)], b_sl[u]], [b_ast[a]])
                    s.dma1("sp", act_sc[pn * 2 + ct], ast[a], b_ast[a], [b_ast[a]], [b_actsc[pn * 2 + ct]])

        P.op("dve", lambda e: e.tensor_copy(out=identb, in_=ident), [b_const], [b_const])

        LNB = 4 if O_P + 6 * 4 * D + 4096 <= ARENA_BYTES else 2
        ln_stage(x_ctx, None, lnin_g, lnin_b, None, None, True, O_P, LNB, late=(gpc, bpc))
        P.barrier()
        proj_small(0, pos_ctx, False, O_S, None, None, None, None)
        P.barrier()
        hgrn(0, False, O_S)
        P.barrier()
        ln_stage(x_main, None, lnin_g, lnin_b, hres, b_hres, True, O_S, 4, late=(gpc, bpc))
        P.barrier()
        cqnT = s.view(O_S, [128, QC, T], BF16)
        tq = s.view(O_S + QC * T * 2, [128, T], F32)
        b_cqn, b_tq = Buf("cqn"), Buf("tq")
        O_M = O_S + QC * T * 2 + 4 * T
        proj_small(1, pos_main, True, O_M, cqnT, b_cqn, tq, b_tq)
        P.barrier()
        attention(O_M, cqnT, b_cqn, tq, b_tq)
        P.barrier()
        hgrn(1, True, O_S)
        P.barrier()
        merge(O_P)
        P.barrier()
        s.dmag("sp", [(hT[:, kc, :], mg_sc[kc]) for kc in range(DC)], b_r1[0], b_mgsc, b_r1)
        O_A = O_S + 2 * NT * 2048
        gaf = s.view(O_A, [128, D], F32)
        baf = s.view(O_A + 4 * D, [128, D], F32)
        assert O_A + 8 * D <= ARENA_BYTES
        b_aff = Buf("aff")
        s.dmag("sp", [(gaf, lnin_g.broadcast_to([128, D])), (baf, lnin_b.broadcast_to([128, D]))], b_aff, [], [b_aff])
        gemm_tm(w_out, DC, hT, lambda t_: [b_r1[t_]], list(range(NT)), 0, hres, b_hres, ysc, b_ysc, O_S, 1,
                aff=(gaf, baf, b_aff))
        P.barrier()
        ln_stage(ysc, b_ysc, ln1_g, ln1_b, h1res, b_h1res, True, O_P, LNB)
        P.barrier()
        ffn1(O_S)
        P.barrier()
        actT = s.view(O_P, [128, DFC, 512], BF16)
        assert O_P + DFC * 1024 <= ARENA_BYTES
        O_F2S = O_R1 if R1_BYTES >= 32768 else O_P + DFC * 1024
        if NH == 2:
            KH = DFC // 2
            actK = s.view(O_P, [128, KH, T], BF16)
            kgs = kch(DFC - KH, 8)
            b_actT = [Buf("actT%d" % i) for i in range(len(kgs))]
            for ps_i, (kbase, kcnt) in enumerate(((0, KH), (KH, DFC - KH))):
                for gi, (k0, nk) in enumerate(kch(kcnt, 8)):
                    s.dmag("sp", [(actK[:, kc, :], act_sc[kbase + kc]) for kc in range(k0, k0 + nk)],
                           b_actT[gi], b_actsc[kbase + k0:kbase + k0 + nk], [b_actT[gi]])
                if ps_i == 0:
                    gemm_tm(w_down, kcnt, actK, None, list(range(NT)), 0, h1res, b_h1res, ysc, b_ysc, O_F2S, 1,
                            kbufs=b_actT, wk0=kbase, alpha=ALPHA)
                else:
                    gemm_tm(w_down, kcnt, actK, None, list(range(NT)), 0, ysc, b_ysc, ysc, b_ysc, O_F2S, 1,
                            kbufs=b_actT, wk0=kbase, alpha=1.0)
        else:
            kgs = kch(DFC, 8)
            b_actT = [Buf("actT%d" % i) for i in range(len(kgs))]
            for th in range(NH):
                for gi, (k0, nk) in enumerate(kgs):
                    s.dmag("sp", [(actT[:, kc, :], act_sc[kc][:, th * 512:(th + 1) * 512])
                                  for kc in range(k0, k0 + nk)], b_actT[gi], b_actsc[k0:k0 + nk], [b_actT[gi]])
                gemm_tm(w_down, DFC, actT, None, [0, 1, 2, 3], th * 512, h1res, b_h1res, ysc, b_ysc,
                        O_F2S, 2, kbufs=b_actT)
        P.barrier()
        fin = ln_stage(ysc, b_ysc, ln2_g, ln2_b, out, [Buf("out%d" % i) for i in range(NT)], False, O_P, LNB)
        P.wait_all("sp", fin)
        print("kernel build: ops=%d dma_sems=%d" % (P.nops, P.nsem))
        P.emit_all()
        s.st.close()
        return nc


def _pc(v, nch):
    return np.ascontiguousarray(np.asarray(v, np.float32).reshape(nch, 128).T)


_CACHE = {}


def make_inputs(cfg, x, positions, ln_in_g, ln_in_b, w_in, q_norm_g, w_uq, kv_norm_g, w_ukv, hg_lb, hg_norm_g,
                w_branch_a, w_branch_b, w_out, ln1_g, ln1_b, w_gate, w_up, w_down, ln2_g, ln2_b):
    T, D = cfg.T, cfg.D
    f32 = lambda a: np.ascontiguousarray(np.asarray(a, np.float32))
    row = lambda a: f32(a).reshape(1, -1)
    p = np.arange(128)
    invf = (10000.0 ** (-(np.arange(32, dtype=np.float32)) / 32.0)).astype(np.float32)[p % 32]
    sgn = np.where((p % 64) < 32, -1.0, 1.0).astype(np.float32)
    kk = np.arange(128)[:, None]
    qq = np.arange(128)[None, :]
    mask_att = np.where((kk >= 64) & (qq < 64), 0.0, 1.0).astype(np.float32)
    mask_hg = np.where((kk // HCH == qq // HCH) & (kk <= qq), 1.0, 0.0).astype(np.float32)
    resetm = np.where(np.arange(T) % HCH == 0, 0.0, 1.0).astype(np.float32)[None, :].repeat(128, 0)
    lb2 = np.asarray(hg_lb, np.float32)
    hglb = np.concatenate([_pc(lb2[0], cfg.HH), _pc(lb2[1], cfg.HH)], axis=1)
    shared = {
        "ident": np.eye(128, dtype=np.float32), "mask_att": mask_att, "mask_hg": mask_hg,
        "resetm": np.ascontiguousarray(resetm),
        "w_in": f32(w_in[0]), "w_uq": f32(w_uq[0]), "w_ukv": f32(w_ukv[0]), "w_a": f32(w_branch_a[0]),
        "w_b": f32(w_branch_b[0]), "w_out": f32(w_out[0]), "w_gate": f32(w_gate[0]), "w_up": f32(w_up[0]),
        "w_down": f32(w_down[0]),
        "lnin_g": row(ln_in_g), "lnin_b": row(ln_in_b), "ln1_g": row(ln1_g[0]), "ln1_b": row(ln1_b[0]),
        "ln2_g": row(ln2_g[0]), "ln2_b": row(ln2_b[0]),
        "qng": _pc(q_norm_g[0], cfg.QC), "kvng": _pc(kv_norm_g[0], cfg.KVC), "hglb": hglb,
        "hgn": _pc(hg_norm_g[0], 1),
        "lnin_gpc": _pc(ln_in_g, cfg.DC), "lnin_bpc": _pc(ln_in_b, cfg.DC),
    }
    x = np.asarray(x, np.float32)
    positions = np.asarray(positions, np.int32)
    maps = []
    B = x.shape[0]
    for core in range(2 * B):
        b, half = core // 2, core % 2
        cst = np.zeros((128, 8), np.float32)
        cst[:, 0] = float(half)
        cst[:, 1] = invf
        cst[:, 3] = sgn
        m = dict(shared)
        m["x_ctx"] = np.ascontiguousarray(x[b, 0:T])
        m["x_main"] = np.ascontiguousarray(x[b, half * T:(half + 1) * T])
        m["pos_ctx"] = np.ascontiguousarray(positions[b, 0:T]).reshape(1, T)
        m["pos_main"] = np.ascontiguousarray(positions[b, half * T:(half + 1) * T]).reshape(1, T)
        m["cst"] = cst
        maps.append(m)
    return maps


def cfg_from_inputs(x, q_norm_g, kv_norm_g, w_uq, hg_lb, w_gate):
    B, SEQ, D = x.shape
    return Cfg(SEQ // 2, D, q_norm_g.shape[1], kv_norm_g.shape[1], w_uq.shape[2] // 192, hg_lb.shape[1] // 128,
               w_gate.shape[2])


def kernel(x, positions, ln_in_g, ln_in_b, w_in, q_norm_g, w_uq, kv_norm_g, w_ukv, hg_lb, hg_norm_g,
           w_branch_a, w_branch_b, w_out, ln1_g, ln1_b, w_gate, w_up, w_down, ln2_g, ln2_b, _debug=()):
    x = np.asarray(x)
    cfg = cfg_from_inputs(x, np.asarray(q_norm_g), np.asarray(kv_norm_g), np.asarray(w_uq), np.asarray(hg_lb),
                          np.asarray(w_gate))
    B, SEQ, D = x.shape
    assert B * 2 == 8
    nc = Builder(cfg, debug=_debug).build()
    maps = make_inputs(cfg, x, positions, ln_in_g, ln_in_b, w_in, q_norm_g, w_uq, kv_norm_g, w_ukv, hg_lb,
                       hg_norm_g, w_branch_a, w_branch_b, w_out, ln1_g, ln1_b, w_gate, w_up, w_down, ln2_g, ln2_b)
    res = run_bass_kernel_spmd(nc, maps, core_ids=list(range(8)))
    T = cfg.T
    outp = np.empty((B, SEQ, D), np.float32)
    for core in range(8):
        b, half = core // 2, core % 2
        outp[b, half * T:(half + 1) * T] = res.results[core]["out"]
    if _debug:
        return outp, res.results
    return outp
```
